# Optimizing a Trainium2 kernel written in Bass

```python
import math
import jax, jax.numpy as jnp
from jax import lax
import numpy as np

D_MODEL = 1024
BATCH = 8
SEQ = 4096
DEPTH = 2

D_MIX = D_MODEL
D_FOURIER = D_MIX // 4
N_FOURIER_HEADS = 4
FOURIER_HEAD_DIM = D_FOURIER // N_FOURIER_HEADS
D_DIFF = D_MIX // 2
N_DIFF_HEADS = 4
DIFF_HEAD_DIM = D_DIFF // (2 * N_DIFF_HEADS)
DIFF_V_DIM = 2 * DIFF_HEAD_DIM
D_POOL = D_MIX - D_FOURIER - D_DIFF
POOL_WINDOWS = (2, 4, 8, 16)
N_POOL_GROUPS = len(POOL_WINDOWS)
POOL_GROUP_DIM = D_POOL // N_POOL_GROUPS
D_IN_PROJ = D_FOURIER + 3 * D_DIFF + D_POOL
D_FF = ((8 * D_MODEL // 3 + 127) // 128) * 128
N_SUBLAYERS = 3
ROPE_THETA = 10000.0
Q_BLOCK = 128
NORM_EPS = 1e-6
SUBLN_EPS = 1e-5
MACARON_WEIGHT = 0.5

kernel_name = "hybrid_fourier_diffattn_pool_macaron_encoder"


def rms_norm(x, g, eps=NORM_EPS):
    xf = x.astype(jnp.float32)
    y = xf * lax.rsqrt(jnp.mean(xf * xf, axis=-1, keepdims=True) + eps)
    return (y * g.astype(jnp.float32)).astype(x.dtype)


def modulate(h, shift, scale):
    return h * (1 + scale[:, None, :]) + shift[:, None, :]


def rope_tables(positions):
    inv = 1.0 / (ROPE_THETA ** (jnp.arange(0, DIFF_HEAD_DIM, 2, dtype=jnp.float32) / DIFF_HEAD_DIM))
    ang = positions.astype(jnp.float32)[:, None] * inv[None, :]
    ang = jnp.concatenate([ang, ang], axis=-1)
    return jnp.cos(ang), jnp.sin(ang)


def apply_rope(t, cos, sin):
    t1, t2 = jnp.split(t, 2, axis=-1)
    rot = jnp.concatenate([-t2, t1], axis=-1)
    return (t * cos[None, :, None, :] + rot * sin[None, :, None, :]).astype(t.dtype)


def swiglu(h, w_gu, w_down):
    g, u = jnp.split(h @ w_gu, 2, axis=-1)
    return (jax.nn.silu(g) * u) @ w_down


def fourier_mixer(u, w_f):
    B, S, _ = u.shape
    uh = u.reshape(B, S, N_FOURIER_HEADS, FOURIER_HEAD_DIM).astype(jnp.float32)
    f = jnp.fft.fft2(uh, axes=(1, 3), norm="ortho").real
    return f.reshape(B, S, D_FOURIER).astype(u.dtype) @ w_f


def diff_attention(q, k, v, lam, lambda_init, g_subln, cos, sin):
    B, S, _ = q.shape
    H, d = N_DIFF_HEADS, DIFF_HEAD_DIM
    q = apply_rope(q.reshape(B, S, 2 * H, d), cos, sin)
    k = apply_rope(k.reshape(B, S, 2 * H, d), cos, sin)
    q = q.reshape(B, S, H, 2, d).transpose(0, 2, 3, 1, 4)
    k = k.reshape(B, S, H, 2, d).transpose(0, 2, 3, 1, 4)
    v = v.reshape(B, S, H, DIFF_V_DIM).transpose(0, 2, 1, 3)
    nblk = S // Q_BLOCK
    qb = q.reshape(B, H, 2, nblk, Q_BLOCK, d).transpose(3, 0, 1, 2, 4, 5)
    scale = d ** -0.5

    def block(q_blk):
        s = jnp.einsum('bhmqd,bhmkd->bhmqk', q_blk, k).astype(jnp.float32) * scale
        p = jax.nn.softmax(s, axis=-1)
        a = p[:, :, 0] - lam * p[:, :, 1]
        return jnp.einsum('bhqk,bhkv->bhqv', a.astype(v.dtype), v)

    o = lax.map(block, qb)
    o = o.transpose(1, 0, 3, 2, 4).reshape(B, S, H, DIFF_V_DIM)
    o = rms_norm(o, g_subln, SUBLN_EPS) * (1.0 - lambda_init)
    return o.reshape(B, S, D_DIFF)


def pool_mixer(u, w_pool, pool_scale):
    B, S, _ = u.shape
    uf = u.astype(jnp.float32)
    pos = jnp.arange(S)
    outs = []
    for g, w in enumerate(POOL_WINDOWS):
        xg = uf[..., g * POOL_GROUP_DIM:(g + 1) * POOL_GROUP_DIM]
        half = w // 2
        xp = jnp.pad(xg, ((0, 0), (half + 1, half), (0, 0)))
        cs = jnp.cumsum(xp, axis=1)
        win_sum = cs[:, w:w + S] - cs[:, 0:S]
        count = (jnp.minimum(pos + half, S) - jnp.maximum(pos - half, 0)).astype(jnp.float32)
        outs.append(win_sum / count[None, :, None] - xg)
    pooled = jnp.stack(outs, axis=2)
    y = jnp.einsum('bsgc,gcd->bsgd', pooled.astype(u.dtype), w_pool).reshape(B, S, D_POOL)
    return y * pool_scale


def setup_inputs(seed: int = 0) -> dict:
    key = jax.random.key(seed)
    ks = jax.random.split(key, 20)
    L, D = DEPTH, D_MODEL

    def dense(k, shape, fan_in):
        return jax.random.normal(k, shape, jnp.float32) * (fan_in ** -0.5)

    def noise(k, shape, s):
        return s * jax.random.normal(k, shape, jnp.float32)

    return {
        "x": jax.random.normal(ks[0], (BATCH, SEQ, D), jnp.float32),
        "c": jax.random.normal(ks[1], (BATCH, D), jnp.float32),
        "positions": jnp.arange(SEQ, dtype=jnp.int32),
        "w_ada": dense(ks[2], (L, D, N_SUBLAYERS * 3 * D), D),
        "b_ada": noise(ks[3], (L, N_SUBLAYERS * 3 * D), 0.01),
        "g_pre": 1.0 + noise(ks[4], (L, N_SUBLAYERS, D), 0.02),
        "g_post": 1.0 + noise(ks[5], (L, N_SUBLAYERS, D), 0.02),
        "w_ff_gu": dense(ks[6], (L, 2, D, 2 * D_FF), D),
        "w_ff_down": dense(ks[7], (L, 2, D_FF, D), D_FF),
        "w_in": dense(ks[8], (L, D, D_IN_PROJ), D),
        "w_fourier": dense(ks[9], (L, D_FOURIER, D_FOURIER), D_FOURIER),
        "lambda_q1": noise(ks[10], (L, DIFF_HEAD_DIM), 0.1),
        "lambda_k1": noise(ks[11], (L, DIFF_HEAD_DIM), 0.1),
        "lambda_q2": noise(ks[12], (L, DIFF_HEAD_DIM), 0.1),
        "lambda_k2": noise(ks[13], (L, DIFF_HEAD_DIM), 0.1),
        "g_subln": 1.0 + noise(ks[14], (L, DIFF_V_DIM), 0.02),
        "w_pool": dense(ks[15], (L, N_POOL_GROUPS, POOL_GROUP_DIM, POOL_GROUP_DIM), POOL_GROUP_DIM),
        "pool_scale": 1.0 + noise(ks[16], (L, D_POOL), 0.1),
        "w_out": dense(ks[17], (L, D_MIX, D), D_MIX),
    }


def reference(x, c, positions, w_ada, b_ada, g_pre, g_post, w_ff_gu, w_ff_down, w_in,
              w_fourier, lambda_q1, lambda_k1, lambda_q2, lambda_k2, g_subln, w_pool,
              pool_scale, w_out):
    B = x.shape[0]
    D = D_MODEL
    cos, sin = rope_tables(positions)
    c_act = jax.nn.silu(c)
    splits = [D_FOURIER, D_FOURIER + D_DIFF, D_FOURIER + 2 * D_DIFF, D_FOURIER + 3 * D_DIFF]
    for l in range(DEPTH):
        ada = (c_act @ w_ada[l] + b_ada[l]).reshape(B, N_SUBLAYERS, 3, D)
        shift, scale, gate = ada[:, :, 0], ada[:, :, 1], ada[:, :, 2]

        h = modulate(rms_norm(x, g_pre[l, 0]), shift[:, 0], scale[:, 0])
        y = swiglu(h, w_ff_gu[l, 0], w_ff_down[l, 0])
        x = x + MACARON_WEIGHT * gate[:, 0][:, None, :] * rms_norm(y, g_post[l, 0])

        h = modulate(rms_norm(x, g_pre[l, 1]), shift[:, 1], scale[:, 1])
        proj = h @ w_in[l]
        u_f, q, k, v, u_p = jnp.split(proj, splits, axis=-1)
        lambda_init = 0.8 - 0.6 * math.exp(-0.3 * l)
        lam = (jnp.exp(jnp.sum(lambda_q1[l].astype(jnp.float32) * lambda_k1[l].astype(jnp.float32)))
               - jnp.exp(jnp.sum(lambda_q2[l].astype(jnp.float32) * lambda_k2[l].astype(jnp.float32)))
               + lambda_init)
        y_f = fourier_mixer(u_f, w_fourier[l])
        y_d = diff_attention(q, k, v, lam, lambda_init, g_subln[l], cos, sin)
        y_p = pool_mixer(u_p, w_pool[l], pool_scale[l])
        y = jnp.concatenate([y_f, y_d, y_p], axis=-1) @ w_out[l]
        x = x + gate[:, 1][:, None, :] * rms_norm(y, g_post[l, 1])

        h = modulate(rms_norm(x, g_pre[l, 2]), shift[:, 2], scale[:, 2])
        y = swiglu(h, w_ff_gu[l, 1], w_ff_down[l, 1])
        x = x + MACARON_WEIGHT * gate[:, 2][:, None, :] * rms_norm(y, g_post[l, 2])
    return x
```

```python
import math
from contextlib import ExitStack
import numpy as np
import ml_dtypes
import concourse.bass as bass
import concourse.mybir as mybir
from concourse.bass_utils import run_bass_kernel_spmd

F32 = mybir.dt.float32
BF16 = mybir.dt.bfloat16
I32 = mybir.dt.int32
AF = mybir.ActivationFunctionType
ALU = mybir.AluOpType
AX = mybir.AxisListType

D = 1024
DFF = 2816
NJ = DFF // 128
DEPTH = 2
SEQ = 4096
NCORES = 8
NORM_EPS = 1e-6
SUBLN_EPS = 1e-5
TWO_PI = 2.0 * math.pi
CW1 = 6.28125
CW2 = TWO_PI - CW1


class Buf:
    __slots__ = ("name", "w", "r", "slot", "psum", "sslot")

    def __init__(self, name, psum=False):
        self.name = name
        self.w = None
        self.r = []
        self.slot = None
        self.sslot = None
        self.psum = psum


class Eng:
    def __init__(self, name, h, sem):
        self.name, self.h, self.sem = name, h, sem
        self.cnt = 0
        self.waited = {}


class K:
    def __init__(self, nc, stack):
        self.nc = nc
        self.stack = stack
        self.nsem = 0
        mk = lambda n, h: Eng(n, h, self._new_sem("e_" + n))
        self.pe = mk("pe", nc.tensor)
        self.act = mk("act", nc.scalar)
        self.dve = mk("dve", nc.vector)
        self.pool = mk("pool", nc.gpsimd)
        self.sp = mk("sp", nc.sync)
        self.engines = [self.pe, self.act, self.dve, self.pool, self.sp]
        self.free_slots = []
        self.live_slots = []
        self.ninst = 0

    def _new_sem(self, name):
        self.nsem += 1
        return self.stack.enter_context(self.nc.semaphore(f"{name}_{self.nsem}"))

    def _need(self, E, reads, writes):
        need = {}

        def add(ev, same_ok):
            if ev is None:
                return
            sem, val = ev
            if same_ok and sem is E.sem:
                return
            kk = id(sem)
            if kk not in need or need[kk][1] < val:
                need[kk] = (sem, val)
        for b in reads:
            add(b.w, False)
            if b.psum:
                for ev in b.r:
                    add(ev, True)
        for b in writes:
            add(b.w, True)
            for ev in b.r:
                add(ev, True)
        for kk, (sem, val) in need.items():
            if E.waited.get(kk, 0) < val:
                E.h.wait_ge(sem, val)
                E.waited[kk] = val

    def _done(self, ev, reads, writes):
        for b in reads:
            b.r.append(ev)
            if len(b.r) > 48:
                d = {}
                for s, v in b.r:
                    if id(s) not in d or d[id(s)][1] < v:
                        d[id(s)] = (s, v)
                b.r = list(d.values())
        for b in writes:
            b.w = ev
            b.r = []

    def op(self, E, fn, reads=(), writes=()):
        self._need(E, reads, writes)
        ins = fn()
        E.cnt += 1
        ins.then_inc(E.sem, 1)
        self._done((E.sem, E.cnt), reads, writes)
        self.ninst += 1

    def mm(self, mms, reads, writes):
        E = self.pe
        self._need(E, reads, writes)
        ins = None
        for fn in mms:
            ins = fn()
            self.ninst += 1
        E.cnt += 1
        ins.then_inc(E.sem, 1)
        self._done((E.sem, E.cnt), reads, writes)

    def dma(self, Q, out, in_, reads, dst, nowaw=False, **kw):
        self._need(Q, reads, [] if nowaw else [dst])
        own = reads[0] if (nowaw and len(reads) > 0) else None
        if own is not None:
            if own.sslot is None:
                own.sslot = self.free_slots.pop() if self.free_slots else [self._new_sem("s"), 0]
                self.live_slots.append(own.sslot)
            slot = own.sslot
        else:
            if dst.slot is None:
                dst.slot = self.free_slots.pop() if self.free_slots else [self._new_sem("d"), 0]
                self.live_slots.append(dst.slot)
            slot = dst.slot
        ins = Q.h.dma_start(out=out, in_=in_, **kw)
        slot[1] += 16
        ins.then_inc(slot[0], 16)
        ev = (slot[0], slot[1])
        self._done(ev, reads, [])
        dst.w = ev
        if not nowaw:
            dst.r = []
        self.ninst += 1

    def barrier(self):
        evs = [(e.sem, e.cnt) for e in self.engines if e.cnt > 0]
        evs += [(s[0], s[1]) for s in self.live_slots if s[1] > 0]
        for E in self.engines:
            for sem, val in evs:
                if sem is E.sem:
                    continue
                if E.waited.get(id(sem), 0) < val:
                    E.h.wait_ge(sem, val)
                    E.waited[id(sem)] = val
        self.free_slots.extend(self.live_slots)
        self.live_slots = []


class Ring:
    def __init__(self, items):
        self.items = items
        self.i = 0

    def next(self):
        it = self.items[self.i % len(self.items)]
        self.i += 1
        return it


class Prog:
    def __init__(self, S, dbg=False):
        self.S = S
        self.NT = S // 512
        self.NCH = S // 128
        self.dbg = dbg
        nc = self.nc = bass.Bass("TRN2", target_bir_lowering=False)
        din = lambda n, s, dt=F32: nc.dram_tensor(n, list(s), dt, kind="ExternalInput").ap()
        dsc = lambda n, s, dt=F32: nc.dram_tensor(n, list(s), dt, kind=("ExternalOutput" if dbg else "Internal")).ap()
        self.x_d = din("x", [S, D])
        self.c_fm = din("c_fm", [128, 8])
        self.pos_d = din("pos", [128, S], I32)
        self.inv_d = din("invf", [128, 1])
        self.w_ada = din("w_ada", [DEPTH, D, 9 * D])
        self.b_ada = din("b_ada", [DEPTH, 9 * D])
        self.b_ada_fm = din("b_ada_fm", [128, DEPTH, 72])
        self.g_pre_fm = din("g_pre_fm", [128, DEPTH * 3, 8])
        self.g_post = din("g_post", [DEPTH * 3, D])
        self.w_gu = din("w_ff_gu", [DEPTH, 2, D, 2 * DFF])
        self.w_dn = din("w_ff_down", [DEPTH, 2, DFF, D])
        self.w_in = din("w_in", [DEPTH, D, 2048])
        self.w_f = din("w_fourier", [DEPTH, 256, 256])
        self.lamv = din("lamv", [DEPTH, 4 * 64])
        self.g_sub = din("g_subln", [DEPTH, 128])
        self.w_pool = din("w_pool", [DEPTH, 4, 64, 64])
        self.pscale_fm = din("pscale_fm", [128, DEPTH, 2])
        self.w_out = din("w_out", [DEPTH, D, D])
        self.ident_d = din("ident", [128, 128], BF16)
        self.wcs_d = din("wcs", [256, 512], BF16)
        self.dft_d = din("dft", [self.NT, self.NCH, 128, 2, 512], BF16)
        self.ntype = 1 if self.NT == 1 else (2 if self.NT == 2 else 3)
        self.band_d = din("band", [self.ntype, 128, 4, 6, 512], BF16)
        self.out_d = nc.dram_tensor("out", [S, D], F32, kind="ExternalOutput").ap()
        self.xs = dsc("xs", [S, D])
        self.GG = dsc("GG", [DEPTH * 3, 128, D])
        self.CS = dsc("CS", [2, 128, S])
        self.QT = dsc("QT", [4, 128, S], BF16)
        self.KT = dsc("KT", [4, 128, S], BF16)
        self.Vd = dsc("Vd", [S, 512], BF16)
        self.ABd = dsc("ABd", [S, 512], BF16)
        self.UPd = dsc("UPd", [S, 256], BF16)
        self.CAT = dsc("CAT", [8, 128, S], BF16)
        self.dram_bufs = {}

    def dbuf(self, name):
        if name not in self.dram_bufs:
            self.dram_bufs[name] = Buf("dram_" + name)
        b = self.dram_bufs[name]
        return b

    def scope(self):
        st = ExitStack()
        nc = self.nc
        cnt = [0]

        def T(shape, dt, name=None):
            cnt[0] += 1
            t = st.enter_context(nc.sbuf_tensor(f"{name or 't'}_{self.phase}_{cnt[0]}", list(shape), dt))
            return t, Buf(f"{name}_{cnt[0]}")

        def P(shape, dt, name=None):
            cnt[0] += 1
            t = st.enter_context(nc.psum_tensor(f"{name or 'p'}_{self.phase}_{cnt[0]}", list(shape), dt))
            return t, Buf(f"ps{name}_{cnt[0]}", psum=True)
        return st, T, P

    def end_phase(self, st):
        self.k.barrier()
        st.close()
        for b in self.dram_bufs.values():
            b.slot = None
            b.w = None
            b.r = []

    def build(self, layers=DEPTH, phases=None):
        nc = self.nc
        with ExitStack() as top:
            self.k = k = K(nc, top)
            self.phase = "top"
            cnt = [0]

            def TT(shape, dt, name):
                cnt[0] += 1
                return top.enter_context(nc.sbuf_tensor(f"g_{name}", list(shape), dt)), Buf(name)
            self.ident, self.b_ident = TT([128, 128], BF16, "ident")
            self.mhalf, self.b_mhalf = TT([128, 1], F32, "mhalf")
            self.ones_row, self.b_ones = TT([1, 128], F32, "ones_row")
            self.modT, self.b_modT = TT([128, DEPTH * 3 * 2, 8], F32, "modT")
            self.lam_bc, self.b_lam = TT([128, DEPTH], F32, "lam_bc")
            self.gsub_bc, self.b_gsub = TT([128, DEPTH, 128], F32, "gsub_bc")
            self.pscale, self.b_pscale = TT([128, DEPTH, 2], F32, "pscale")
            k.dma(k.sp, self.ident[:], self.ident_d[:], [], self.b_ident)
            k.dma(k.sp, self.pscale[:], self.pscale_fm[:], [], self.b_pscale)
            k.op(k.pool, lambda: nc.gpsimd.memset(self.mhalf[:], -0.5), [], [self.b_mhalf])
            k.op(k.pool, lambda: nc.gpsimd.memset(self.ones_row[:], 1.0), [], [self.b_ones])
            want = lambda p: phases is None or p in phases
            def outer_scope():
                stx = ExitStack()
                c = [0]

                def TX(shape, dt, name=None):
                    c[0] += 1
                    self.pfx = getattr(self, "pfx", 0) + 1
                    t = stx.enter_context(nc.sbuf_tensor(f"pf_{name}_{self.pfx}", list(shape), dt))
                    return t, Buf(f"pf_{name}_{self.pfx}")
                return stx, TX
            full = phases is None
            W0 = None
            stA = None
            if full:
                stA, TA = outer_scope()
                W0 = {}
                self.load_wgu(0, 0, TA, W0)
                self.load_wdn(0, 0, TA, W0)
            if want("p0"):
                self.phase_p0()
            for l in range(layers):
                first = (l == 0)
                last = (l == layers - 1)
                if want("f1"):
                    self.phase_ffn(l, 0, src=(self.x_d if first else self.xs), dst=self.xs, W=(W0 if first else None))
                if first and stA is not None:
                    stA.close()
                if want("m1"):
                    self.phase_m1(l)
                if want("m3a"):
                    self.phase_fourier(l)
                if want("m3b"):
                    self.phase_pool(l)
                W2 = None
                stB = None
                if full:
                    stB, TB = outer_scope()
                    W2 = {}
                    self.load_wgu(l, 1, TB, W2)
                if want("m2"):
                    self.phase_attn(l)
                stC = None
                if full:
                    issue_wdn = self.load_wdn(l, 1, TB, W2, defer=True)
                    stC, TC = outer_scope()
                    self.load_m4(l, TC, W2)
                    issue_wdn()
                if want("m4"):
                    self.phase_m4(l, W2)
                if stC is not None:
                    stC.close()
                if want("f2"):
                    self.phase_ffn(l, 1, src=self.xs, dst=(self.out_d if last else self.xs), W=W2)
                if stB is not None:
                    stB.close()
            k.barrier()
        return nc

    def phase_p0(self):
        nc, k, S = self.nc, self.k, self.S
        self.phase = "p0"
        st, T, P = self.scope()
        CHW = min(S, 512)
        posi, b_posi = T([128, CHW], I32, "posi")
        ang, b_ang = T([128, CHW], F32, "ang")
        t1, b_t1 = T([128, CHW], F32, "t1")
        ni, b_ni = T([128, CHW], I32, "ni")
        nf, b_nf = T([128, CHW], F32, "nf")
        invt, b_inv = T([128, 1], F32, "invt")
        b_CS = self.dbuf("CS")
        k.dma(k.act, invt[:], self.inv_d[:], [], b_inv)
        for c0 in range(0, S, CHW):
            k.dma(k.act, posi[:], self.pos_d[:, c0:c0 + CHW], [], b_posi)
            k.op(k.dve, lambda: nc.vector.tensor_copy(ang[:], posi[:]), [b_posi], [b_ang])
            k.op(k.dve, lambda: nc.vector.tensor_scalar(ang[:], ang[:], invt[:, 0:1], None, ALU.mult), [b_ang, b_inv], [b_ang])
            for which in range(2):
                off = (math.pi / 2) if which == 0 else 0.0
                k.op(k.dve, lambda off=off: nc.vector.tensor_scalar(t1[:], ang[:], off, 1.0 / TWO_PI, ALU.add, ALU.mult), [b_ang], [b_t1])
                k.op(k.dve, lambda: nc.vector.tensor_copy(ni[:], t1[:]), [b_t1], [b_ni])
                k.op(k.dve, lambda: nc.vector.tensor_copy(nf[:], ni[:]), [b_ni], [b_nf])
                k.op(k.dve, lambda off=off: nc.vector.tensor_scalar(t1[:], ang[:], off, None, ALU.add), [b_ang], [b_t1])
                k.op(k.dve, lambda: nc.vector.scalar_tensor_tensor(t1[:], nf[:], -CW1, t1[:], ALU.mult, ALU.add), [b_nf, b_t1], [b_t1])
                k.op(k.dve, lambda: nc.vector.scalar_tensor_tensor(t1[:], nf[:], -CW2, t1[:], ALU.mult, ALU.add), [b_nf, b_t1], [b_t1])
                k.op(k.dve, lambda: nc.vector.tensor_scalar(t1[:], t1[:], -math.pi, math.pi, ALU.max, ALU.min), [b_t1], [b_t1])
                k.op(k.act, lambda: nc.scalar.activation(nf[:], t1[:], AF.Sin), [b_t1], [b_nf])
                k.dma(k.act, self.CS[which, :, c0:c0 + CHW], nf[:], [b_nf], b_CS, nowaw=True)
        cT, b_cT = T([128, 8], F32, "cT")
        cact, b_cact = T([128, 8], F32, "cact")
        crep, b_crep = T([128, 8, 128], F32, "crep")
        bfm, b_bfm = T([128, DEPTH, 72], F32, "bfm")
        gpre, b_gpre = T([128, DEPTH * 3, 8], F32, "gpre")
        k.dma(k.sp, cT[:], self.c_fm[:], [], b_cT)
        k.dma(k.sp, bfm[:], self.b_ada_fm[:], [], b_bfm)
        k.dma(k.sp, gpre[:], self.g_pre_fm[:], [], b_gpre)
        k.op(k.act, lambda: nc.scalar.activation(cact[:], cT[:], AF.Silu), [b_cT], [b_cact])
        for kk in range(8):
            k.op(k.dve, lambda kk=kk: nc.vector.tensor_copy(crep[:, kk, :], cact[:, kk:kk + 1].to_broadcast([128, 128])),
                 [b_cact], [b_crep])
        wp = Ring([T([128, 8, 512], F32, "wp") for _ in range(2)])
        brow = Ring([T([1, 512], F32, "brow") for _ in range(2)])
        grow = Ring([T([1, 512], F32, "grow") for _ in range(2)])
        gpsb = Ring([T([128, 512], F32, "gpsb") for _ in range(2)])
        ggt = Ring([T([128, 512], F32, "ggt") for _ in range(2)])
        tmp8 = Ring([T([128, 8], F32, "tmp8") for _ in range(2)])
        ps_fm = Ring([P([128, 512], F32, "psfm") for _ in range(2)])
        ps_g = Ring([P([128, 512], F32, "psg") for _ in range(2)])
        ps_p = Ring([P([128, 512], F32, "psp") for _ in range(2)])
        b_GG = self.dbuf("GG")
        for l in range(DEPTH):
            for s in range(3):
                for j in range(3):
                    v = s * 3 + j
                    if j < 2:
                        pfm, b_pfm = ps_fm.next()
                    for hh in range(2):
                        w_t, b_w = wp.next()
                        c0 = v * D + 512 * hh
                        k.dma(k.sp, w_t[:], self.w_ada[l, :, c0:c0 + 512].rearrange("(kk p) c -> p kk c", p=128), [], b_w)
                        if j < 2:
                            for cch in range(4):
                                col = 4 * hh + cch
                                k.mm([(lambda kk=kk, cch=cch, col=col, w_t=w_t, pfm=pfm: nc.tensor.matmul(
                                    pfm[:, col:col + 1], w_t[:, kk, 128 * cch:128 * cch + 128], cact[:, kk:kk + 1],
                                    start=(kk == 0), stop=(kk == 7))) for kk in range(8)],
                                    [b_w, b_cact], [b_pfm])
                        else:
                            br, b_br = brow.next()
                            gr, b_gr = grow.next()
                            k.dma(k.sp, br[:], self.b_ada[l:l + 1, c0:c0 + 512], [], b_br)
                            k.dma(k.sp, gr[:], self.g_post[l * 3 + s:l * 3 + s + 1, 512 * hh:512 * hh + 512], [], b_gr)
                            pg, b_pg = ps_g.next()
                            pp, b_pp = ps_p.next()
                            mms = [(lambda kk=kk, w_t=w_t, pg=pg: nc.tensor.matmul(
                                pg[:], crep[:, kk, :], w_t[:, kk, :], start=(kk == 0), stop=False)) for kk in range(8)]
                            mms.append(lambda br=br, pg=pg: nc.tensor.matmul(pg[:], self.ones_row[0:1, :], br[0:1, :], start=False, stop=True))
                            k.mm(mms, [b_w, b_crep, b_br, self.b_ones], [b_pg])
                            k.mm([lambda gr=gr, pp=pp: nc.tensor.matmul(pp[:], self.ones_row[0:1, :], gr[0:1, :], start=True, stop=True)],
                                 [b_gr, self.b_ones], [b_pp])
                            gs_t, b_gs = gpsb.next()
                            k.op(k.act, lambda gs_t=gs_t, pp=pp: nc.scalar.copy(gs_t[:], pp[:]), [b_pp], [b_gs])
                            gg_t, b_gg = ggt.next()
                            mw = 1.0 if s == 1 else 0.5
                            k.op(k.dve, lambda gg_t=gg_t, pg=pg, gs_t=gs_t, mw=mw: nc.vector.scalar_tensor_tensor(
                                gg_t[:], pg[:], mw, gs_t[:], ALU.mult, ALU.mult), [b_pg, b_gs], [b_gg])
                            k.dma(k.sp, self.GG[l * 3 + s, :, 512 * hh:512 * hh + 512], gg_t[:], [b_gg], b_GG, nowaw=True)
                    if j < 2:
                        idx = (l * 3 + s) * 2
                        if j == 0:
                            k.op(k.dve, lambda pfm=pfm, idx=idx, l=l, v=v: nc.vector.tensor_tensor(
                                self.modT[:, idx + 1, :], pfm[:, 0:8], bfm[:, l, v * 8:v * 8 + 8], ALU.add),
                                [b_pfm, b_bfm], [self.b_modT])
                        else:
                            t8, b_t8 = tmp8.next()
                            k.op(k.dve, lambda pfm=pfm, t8=t8, l=l, v=v: nc.vector.tensor_tensor(
                                t8[:], pfm[:, 0:8], bfm[:, l, v * 8:v * 8 + 8], ALU.add), [b_pfm, b_bfm], [b_t8])
                            k.op(k.dve, lambda t8=t8, idx=idx, l=l, s=s: nc.vector.scalar_tensor_tensor(
                                self.modT[:, idx, :], t8[:], 1.0, gpre[:, l * 3 + s, :], ALU.add, ALU.mult),
                                [b_t8, b_gpre], [self.b_modT])
        lamr, b_lamr = T([1, DEPTH, 256], F32, "lamr")
        lprod, b_lprod = T([1, DEPTH, 2, 64], F32, "lprod")
        lsum, b_lsum = T([1, DEPTH, 2], F32, "lsum")
        lexp, b_lexp = T([1, DEPTH, 2], F32, "lexp")
        lval, b_lval = T([1, DEPTH], F32, "lval")
        gsr, b_gsr = T([1, DEPTH, 128], F32, "gsr")
        k.dma(k.sp, lamr[:], self.lamv.rearrange("(o l) f -> o l f", o=1), [], b_lamr)
        k.dma(k.sp, gsr[:], self.g_sub.rearrange("(o l) f -> o l f", o=1), [], b_gsr)
        for l in range(DEPTH):
            lam_init = 0.8 - 0.6 * math.exp(-0.3 * l)
            for m in range(2):
                k.op(k.dve, lambda l=l, m=m: nc.vector.tensor_tensor(
                    lprod[:, l, m, :], lamr[:, l, 128 * m:128 * m + 64], lamr[:, l, 128 * m + 64:128 * m + 128], ALU.mult),
                    [b_lamr], [b_lprod])
            k.op(k.dve, lambda l=l: nc.vector.reduce_sum(lsum[:, l, :], lprod[:, l, :, :], AX.X), [b_lprod], [b_lsum])
            k.op(k.act, lambda l=l: nc.scalar.activation(lexp[:, l, :], lsum[:, l, :], AF.Exp), [b_lsum], [b_lexp])
            k.op(k.dve, lambda l=l, li=lam_init: nc.vector.scalar_tensor_tensor(
                lval[:, l:l + 1], lexp[:, l, 1:2], -li, lexp[:, l, 0:1], ALU.add, ALU.subtract), [b_lexp], [b_lval])
            pp, b_pp = ps_p.next()
            k.mm([lambda l=l, pp=pp: nc.tensor.matmul(pp[:, 0:1], self.ones_row[0:1, :], lval[0:1, l:l + 1], start=True, stop=True)],
                 [b_lval, self.b_ones], [b_pp])
            k.op(k.act, lambda l=l, pp=pp: nc.scalar.copy(self.lam_bc[:, l:l + 1], pp[:, 0:1]), [b_pp], [self.b_lam])
            pg, b_pg = ps_g.next()
            k.mm([lambda l=l, pg=pg: nc.tensor.matmul(pg[:, 0:128], self.ones_row[0:1, :], gsr[0:1, l, :], start=True, stop=True)],
                 [b_gsr, self.b_ones], [b_pg])
            k.op(k.act, lambda l=l, pg=pg, li=lam_init: nc.scalar.mul(self.gsub_bc[:, l, :], pg[:, 0:128], 1.0 - li), [b_pg], [self.b_gsub])
        self.end_phase(st)

    def pre_A(self, x_t, b_x, R):
        nc, k = self.nc, self.k
        junk, b_junk = R["junk"].next()
        ss, b_ss = R["ss"].next()
        xn, b_xn = R["xn"].next()
        k.op(k.act, lambda: nc.scalar.activation(junk[:], x_t, AF.Square, accum_out=ss[:, 0:1]), [b_x], [b_junk, b_ss])
        k.op(k.dve, lambda: nc.vector.tensor_scalar(ss[:, 1:2], ss[:, 0:1], 1.0 / D, NORM_EPS, ALU.mult, ALU.add), [b_ss], [b_ss])
        k.op(k.pool, lambda: nc.gpsimd.tensor_tensor(ss[:, 2:3], ss[:, 1:2], self.mhalf[:], ALU.pow), [b_ss, self.b_mhalf], [b_ss])
        k.op(k.dve, lambda: nc.vector.tensor_scalar(xn[:], x_t, ss[:, 2:3], None, ALU.mult), [b_x, b_ss], [b_xn])
        return xn, b_xn

    def pre_B(self, l, s, xn, b_xn, hT, b_hT, sub, R):
        nc, k = self.nc, self.k
        pt, b_pt = R["pt"].next()
        k.mm([(lambda c=c: nc.tensor.transpose(pt[:, c, :], xn[:, 128 * c:128 * c + 128], self.ident[:])) for c in range(8)],
             [b_xn, self.b_ident], [b_pt])
        idx = (l * 3 + s) * 2
        for c in range(8):
            if sub % 2 == 0:
                k.op(k.act, lambda c=c: nc.scalar.activation(
                    hT[:, c, 128 * sub:128 * sub + 128], pt[:, c, :], AF.Identity,
                    bias=self.modT[:, idx + 1, c:c + 1], scale=self.modT[:, idx, c:c + 1]), [b_pt, self.b_modT], [b_hT])
            else:
                k.op(k.dve, lambda c=c: nc.vector.tensor_scalar(
                    hT[:, c, 128 * sub:128 * sub + 128], pt[:, c, :], self.modT[:, idx, c:c + 1], self.modT[:, idx + 1, c:c + 1],
                    ALU.mult, ALU.add), [b_pt, self.b_modT], [b_hT])

    def pre_sub(self, l, s, x_t, b_x, hT, b_hT, sub, R):
        xn, b_xn = self.pre_A(x_t, b_x, R)
        self.pre_B(l, s, xn, b_xn, hT, b_hT, sub, R)

    def pre_sched(self, l, s, src, t, hTs, xpre, R):
        k = self.k
        hT, b_hT = hTs.next()
        xns = {}

        def A(sub):
            x_t, b_x = xpre.next()
            r0 = 512 * t + 128 * sub
            k.dma(k.sp, x_t[:], src[r0:r0 + 128, :], [], b_x)
            xns[sub] = self.pre_A(x_t[:], b_x, R)

        def B(sub):
            xn, b_xn = xns[sub]
            self.pre_B(l, s, xn, b_xn, hT, b_hT, sub, R)
        st1 = lambda: (A(0), A(1))
        st2 = lambda: (B(0), B(1), A(2), A(3))
        st3 = lambda: (B(2), B(3))
        return hT, b_hT, [st1, st2, st3]

    def pre_rings(self, T, P):
        return {
            "junk": Ring([T([128, D], BF16, "junk") for _ in range(1)]),
            "ss": Ring([T([128, 4], F32, "ss") for _ in range(4)]),
            "xn": Ring([T([128, D], BF16, "xn") for _ in range(2)]),
            "pt": Ring([P([128, 8, 128], BF16, "pt") for _ in range(1)]),
        }

    def post_sub(self, yh, x_t, b_x, gg, b_gg, R, dst_rows, b_dst):
        nc, k = self.nc, self.k
        import os
        pstop = int(os.environ.get("POST_STOP", "99"))
        junk, b_junk = R["junk"].next()
        ss, b_ss = R["ss"].next()
        tmp, b_tmp = R["tmp"].next()
        if pstop < 1:
            return
        for h in range(2):
            k.op(k.act, lambda h=h: nc.scalar.activation(junk[:, 0:512], yh[h][0], AF.Square, accum_out=ss[:, h:h + 1]),
                 [yh[h][1]], [b_junk, b_ss])
            k.op(k.dve, lambda h=h: nc.vector.tensor_tensor(tmp[:, 512 * h:512 * h + 512], yh[h][0], gg[:, 512 * h:512 * h + 512], ALU.mult),
                 [yh[h][1], b_gg], [b_tmp])
        if pstop < 2:
            return
        k.op(k.dve, lambda: nc.vector.tensor_tensor(ss[:, 2:3], ss[:, 0:1], ss[:, 1:2], ALU.add), [b_ss], [b_ss])
        k.op(k.dve, lambda: nc.vector.tensor_scalar(ss[:, 3:4], ss[:, 2:3], 1.0 / D, NORM_EPS, ALU.mult, ALU.add), [b_ss], [b_ss])
        k.op(k.pool, lambda: nc.gpsimd.tensor_tensor(ss[:, 2:3], ss[:, 3:4], self.mhalf[:], ALU.pow), [b_ss, self.b_mhalf], [b_ss])
        if pstop < 3:
            return
        k.op(k.dve, lambda: nc.vector.scalar_tensor_tensor(x_t, tmp[:], ss[:, 2:3], x_t, ALU.mult, ALU.add), [b_tmp, b_ss, b_x], [b_x])
        if pstop < 4:
            return
        k.dma(k.sp, dst_rows, x_t, [b_x], b_dst, nowaw=True)

    def load_wgu(self, l, f, T, W):
        k = self.k
        wgu, _ = T([128, 8, 2 * DFF], BF16, "wgu")
        groups = [(g0, min(NJ, g0 + 4)) for g0 in range(0, NJ, 4)]
        b_wg = []
        for gi, (g0, g1) in enumerate(groups):
            bg, bu = Buf(f"wg{gi}"), Buf(f"wu{gi}")
            for base, bb in ((0, bg), (DFF, bu)):
                c0, c1 = base + 128 * g0, base + 128 * g1
                k.dma(k.pool, wgu[:, :, c0:c1], self.w_gu[l, f, :, c0:c1].rearrange("(kk p) c -> p kk c", p=128), [], bb)
            b_wg.append((bg, bu))
        W["wgu"], W["b_wg"] = wgu, b_wg

    def load_wdn(self, l, f, T, W, defer=False):
        k = self.k
        wdn, _ = T([128, NJ, D], BF16, "wdn")
        b_wdnh = [Buf(f"wdn{h}") for h in range(2)]

        def issue():
            for h in range(2):
                k.dma(k.pool, wdn[:, 11 * h:11 * h + 11, :],
                      self.w_dn[l, f, 1408 * h:1408 * h + 1408, :].rearrange("(j p) d -> p j d", p=128), [], b_wdnh[h])
        W["wdn"], W["b_wdnh"] = wdn, b_wdnh
        if defer:
            return issue
        issue()

    def phase_ffn(self, l, f, src, dst, W=None):
        nc, k, S = self.nc, self.k, self.S
        s = 0 if f == 0 else 2
        self.phase = f"ffn{l}{f}"
        st, T, P = self.scope()
        if W is None:
            W = {}
            self.load_wgu(l, f, T, W)
            self.load_wdn(l, f, T, W)
        wgu, b_wg, wdn, b_wdnh = W["wgu"], W["b_wg"], W["wdn"], W["b_wdnh"]
        gg, b_gg = T([128, D], F32, "gg")
        k.dma(k.sp, gg[:], self.GG[l * 3 + s], [], b_gg)
        R = self.pre_rings(T, P)
        R["tmp"] = Ring([T([128, D], F32, "tmp") for _ in range(1)])
        xpre = Ring([T([128, D], F32, "xpre") for _ in range(2)])
        xpost = Ring([T([128, D], F32, "xpost") for _ in range(2)])
        hTs = Ring([T([128, 8, 512], BF16, "hT") for _ in range(2)])
        actT, b_act = T([128, NJ, 512], BF16, "actT")
        sg = Ring([T([128, 512], F32, "sg") for _ in range(2)])
        psg = Ring([P([128, 512], F32, "psg") for _ in range(2)])
        psu = Ring([P([128, 512], F32, "psu") for _ in range(2)])
        psy = Ring([P([128, 512], F32, "psy") for _ in range(3)])
        b_dst = self.dbuf("xs_out")

        def do_pre(t):
            hT, b_hT = hTs.next()
            for sub in range(4):
                x_t, b_x = xpre.next()
                r0 = 512 * t + 128 * sub
                k.dma(k.sp, x_t[:], src[r0:r0 + 128, :], [], b_x)
                self.pre_sub(l, s, x_t[:], b_x, hT, b_hT, sub, R)
            return hT, b_hT

        import os
        stop = int(os.environ.get("FFN_STOP", "99"))
        if stop >= 1:
            nxt = do_pre(0)
        for t in range(self.NT if stop >= 2 else 0):
            hT, b_hT = nxt
            for j in range(NJ):
                pg, b_pg = psg.next()
                pu, b_pu = psu.next()
                k.mm([(lambda kk=kk, j=j, pg=pg: nc.tensor.matmul(pg[:], wgu[:, kk, 128 * j:128 * j + 128], hT[:, kk, :],
                                                                  start=(kk == 0), stop=(kk == 7))) for kk in range(8)],
                     [b_hT, b_wg[j // 4][0]], [b_pg])
                k.mm([(lambda kk=kk, j=j, pu=pu: nc.tensor.matmul(pu[:], wgu[:, kk, DFF + 128 * j:DFF + 128 * j + 128], hT[:, kk, :],
                                                                  start=(kk == 0), stop=(kk == 7))) for kk in range(8)],
                     [b_hT, b_wg[j // 4][1]], [b_pu])
                sg_t, b_sg = sg.next()
                k.op(k.act, lambda pg=pg, sg_t=sg_t: nc.scalar.activation(sg_t[:], pg[:], AF.Silu), [b_pg], [b_sg])
                k.op(k.dve, lambda j=j, pu=pu, sg_t=sg_t: nc.vector.tensor_tensor(actT[:, j, :], pu[:], sg_t[:], ALU.mult),
                     [b_pu, b_sg], [b_act])
                if t + 1 < self.NT:
                    if j == 3:
                        nh, nb, stg = self.pre_sched(l, s, src, t + 1, hTs, xpre, R)
                        nxt = (nh, nb)
                        stg[0]()
                    elif j == 9:
                        stg[1]()
                    elif j == 15:
                        stg[2]()
            for sub in range(4 if stop >= 3 else 0):
                yh = []
                for h in range(2):
                    py, b_py = psy.next()
                    k.mm([(lambda j=j, py=py, sub=sub, h=h: nc.tensor.matmul(
                        py[:], actT[:, j, 128 * sub:128 * sub + 128], wdn[:, j, 512 * h:512 * h + 512],
                        start=(j == 0), stop=(j == NJ - 1))) for j in range(NJ)],
                        [b_act] + b_wdnh, [b_py])
                    yh.append((py[:], b_py))
                x_t, b_x = xpost.next()
                r0 = 512 * t + 128 * sub
                k.dma(k.sp, x_t[:], src[r0:r0 + 128, :], [], b_x)
                self.post_sub(yh, x_t[:], b_x, gg, b_gg, R, dst[r0:r0 + 128, :], b_dst)
        self.end_phase(st)

    def phase_m1(self, l):
        nc, k, S = self.nc, self.k, self.S
        self.phase = f"m1{l}"
        st, T, P = self.scope()
        win, b_win = T([128, 8, 2048], BF16, "win")
        wrot, b_wrot = T([128, 8, 1024], BF16, "wrot")
        wcs, b_wcs = T([128, 2, 512], BF16, "wcs")
        b_win_a, b_win_b, b_win_c = Buf("win_a"), Buf("win_b"), Buf("win_c")
        for (c0, c1, bb) in ((0, 256, b_win_a), (256, 1280, b_win_b), (1280, 2048, b_win_c)):
            k.dma(k.pool, win[:, :, c0:c1], self.w_in[l][:, c0:c1].rearrange("(kk p) c -> p kk c", p=128), [], bb)
        k.dma(k.sp, wcs[:], self.wcs_d.rearrange("(c p) f -> p c f", p=128), [], b_wcs)
        for kk in range(8):
            src = win[:, kk, 256:1280].rearrange("p (m t d) -> p m t d", t=2, d=32)
            dstv = wrot[:, kk, :].rearrange("p (m t d) -> p m t d", t=2, d=32)
            k.op(k.dve, lambda src=src, dstv=dstv: nc.vector.tensor_scalar(dstv[:, :, 0, :], src[:, :, 1, :], -1.0, None, ALU.mult),
                 [b_win_b], [b_wrot])
            k.op(k.dve, lambda src=src, dstv=dstv: nc.vector.tensor_copy(dstv[:, :, 1, :], src[:, :, 0, :]), [b_win_b], [b_wrot])
        R = self.pre_rings(T, P)
        xpre = Ring([T([128, D], F32, "xpre") for _ in range(2)])
        hTs = Ring([T([128, 8, 512], BF16, "hT") for _ in range(2)])
        cs = Ring([T([128, 2, 512], F32, "cs") for _ in range(2)])
        ufT = Ring([T([128, 2, 512], BF16, "ufT") for _ in range(2)])
        ab = Ring([T([128, 4, 512], BF16, "ab") for _ in range(2)])
        qk = Ring([T([128, 8, 512], BF16, "qk") for _ in range(2)])
        vv = Ring([T([128, 4, 512], BF16, "vv") for _ in range(2)])
        up = Ring([T([128, 4, 256], BF16, "up") for _ in range(2)])
        r1 = Ring([T([128, 512], F32, "r1") for _ in range(2)])
        r2 = Ring([T([128, 512], F32, "r2") for _ in range(2)])
        psa = Ring([P([128, 512], F32, "psa") for _ in range(3)])
        psb = Ring([P([128, 512], F32, "psb") for _ in range(3)])
        bQT, bKT, bV, bAB, bUP = (self.dbuf(n) for n in ("QT", "KT", "V", "AB", "UP"))
        src_x = self.xs
        nh, nb, stg = self.pre_sched(l, 1, src_x, 0, hTs, xpre, R)
        for f_ in stg:
            f_()
        nxt = (nh, nb)
        for t in range(self.NT):
            hT, b_hT = nxt
            stg = None
            if t + 1 < self.NT:
                nh, nb, stg = self.pre_sched(l, 1, src_x, t + 1, hTs, xpre, R)
                nxt = (nh, nb)
                stg[0]()
            cs_t, b_cs = cs.next()
            k.dma(k.sp, cs_t[:], self.CS[:, :, 512 * t:512 * t + 512].rearrange("w p s -> p w s"), [], b_cs)
            uf, b_uf = ufT.next()
            for cc in range(2):
                pa, b_pa = psa.next()
                k.mm([(lambda kk=kk, cc=cc, pa=pa: nc.tensor.matmul(pa[:], win[:, kk, 128 * cc:128 * cc + 128], hT[:, kk, :],
                                                                    start=(kk == 0), stop=(kk == 7))) for kk in range(8)],
                     [b_win_a, b_hT], [b_pa])
                k.op(k.act, lambda cc=cc, pa=pa, uf=uf: nc.scalar.copy(uf[:, cc, :], pa[:]), [b_pa], [b_uf])
            ab_t, b_ab = ab.next()
            for sub in range(4):
                pb, b_pb = psb.next()
                k.mm([(lambda cc=cc, sub=sub, pb=pb, uf=uf: nc.tensor.matmul(pb[:], uf[:, cc, 128 * sub:128 * sub + 128], wcs[:, cc, :],
                                                                            start=(cc == 0), stop=(cc == 1))) for cc in range(2)],
                     [b_uf, b_wcs], [b_pb])
                k.op(k.act, lambda sub=sub, pb=pb, ab_t=ab_t: nc.scalar.copy(ab_t[:, sub, :], pb[:]), [b_pb], [b_ab])
            k.dma(k.sp, self.ABd[512 * t:512 * t + 512, :].rearrange("(s p) f -> p s f", p=128), ab_t[:], [b_ab], bAB, nowaw=True)
            qk_t, b_qk = qk.next()
            for c in range(8):
                if stg is not None and c == 0:
                    stg[1]()
                if stg is not None and c == 4:
                    stg[2]()
                pa, b_pa = psa.next()
                pb, b_pb = psb.next()
                k.mm([(lambda kk=kk, c=c, pa=pa: nc.tensor.matmul(pa[:], win[:, kk, 256 + 128 * c:256 + 128 * c + 128], hT[:, kk, :],
                                                                  start=(kk == 0), stop=(kk == 7))) for kk in range(8)],
                     [b_win_b, b_hT], [b_pa])
                k.mm([(lambda kk=kk, c=c, pb=pb: nc.tensor.matmul(pb[:], wrot[:, kk, 128 * c:128 * c + 128], hT[:, kk, :],
                                                                  start=(kk == 0), stop=(kk == 7))) for kk in range(8)],
                     [b_wrot, b_hT], [b_pb])
                a1, b_a1 = r1.next()
                a2, b_a2 = r2.next()
                k.op(k.dve, lambda pa=pa, a1=a1, cs_t=cs_t: nc.vector.tensor_tensor(a1[:], pa[:], cs_t[:, 0, :], ALU.mult), [b_pa, b_cs], [b_a1])
                k.op(k.dve, lambda pb=pb, a2=a2, cs_t=cs_t: nc.vector.tensor_tensor(a2[:], pb[:], cs_t[:, 1, :], ALU.mult), [b_pb, b_cs], [b_a2])
                k.op(k.pool, lambda c=c, a1=a1, a2=a2, qk_t=qk_t: nc.gpsimd.tensor_tensor(qk_t[:, c, :], a1[:], a2[:], ALU.add), [b_a1, b_a2], [b_qk])
            k.dma(k.sp, self.QT[:, :, 512 * t:512 * t + 512].rearrange("c p s -> p c s"), qk_t[:, 0:4, :], [b_qk], bQT, nowaw=True)
            k.dma(k.sp, self.KT[:, :, 512 * t:512 * t + 512].rearrange("c p s -> p c s"), qk_t[:, 4:8, :], [b_qk], bKT, nowaw=True)
            v_t, b_v = vv.next()
            u_t, b_u = up.next()
            for sub in range(4):
                pa, b_pa = psa.next()
                pb, b_pb = psb.next()
                k.mm([(lambda kk=kk, sub=sub, pa=pa: nc.tensor.matmul(pa[:], hT[:, kk, 128 * sub:128 * sub + 128], win[:, kk, 1280:1792],
                                                                      start=(kk == 0), stop=(kk == 7))) for kk in range(8)],
                     [b_win_c, b_hT], [b_pa])
                k.mm([(lambda kk=kk, sub=sub, pb=pb: nc.tensor.matmul(pb[:, 0:256], hT[:, kk, 128 * sub:128 * sub + 128], win[:, kk, 1792:2048],
                                                                      start=(kk == 0), stop=(kk == 7))) for kk in range(8)],
                     [b_win_c, b_hT], [b_pb])
                k.op(k.act, lambda sub=sub, pa=pa, v_t=v_t: nc.scalar.copy(v_t[:, sub, :], pa[:]), [b_pa], [b_v])
                k.op(k.dve, lambda sub=sub, pb=pb, u_t=u_t: nc.vector.tensor_copy(u_t[:, sub, :], pb[:, 0:256]), [b_pb], [b_u])
            k.dma(k.sp, self.Vd[512 * t:512 * t + 512, :].rearrange("(s p) f -> p s f", p=128), v_t[:], [b_v], bV, nowaw=True)
            k.dma(k.sp, self.UPd[512 * t:512 * t + 512, :].rearrange("(s p) f -> p s f", p=128), u_t[:], [b_u], bUP, nowaw=True)
        self.end_phase(st)

    def phase_fourier(self, l):
        nc, k, S, NCH = self.nc, self.k, self.S, self.NCH
        self.phase = f"m3a{l}"
        st, T, P = self.scope()
        aball, b_aball = T([128, NCH, 512], BF16, "aball")
        b_abg = [Buf(f"abg{g}") for g in range((NCH + 7) // 8)]
        for g in range(len(b_abg)):
            c0, c1 = 8 * g, min(NCH, 8 * g + 8)
            k.dma(k.sp, aball[:, c0:c1, :], self.ABd[128 * c0:128 * c1, :].rearrange("(c p) f -> p c f", p=128), [], b_abg[g])
        wf, b_wf = T([128, 2, 256], BF16, "wf")
        k.dma(k.pool, wf[:], self.w_f[l].rearrange("(c p) f -> p c f", p=128), [], b_wf)
        PC = 4 if NCH >= 4 else NCH
        tab = Ring([T([128, PC, 2, 512], BF16, "tab") for _ in range(3)])
        fT = Ring([T([128, 2, 512], BF16, "fT") for _ in range(2)])
        yf = Ring([T([128, 2, 512], BF16, "yf") for _ in range(2)])
        psf = Ring([P([128, 512], F32, "psf") for _ in range(4)])
        psy = Ring([P([128, 512], F32, "psy") for _ in range(2)])
        bCAT = self.dbuf("CAT")
        for t in range(self.NT):
            acc = [psf.next() for _ in range(2)]
            npc = NCH // PC
            for pc in range(npc):
                tb, b_tb = tab.next()
                k.dma(k.sp, tb[:], self.dft_d[t, pc * PC:(pc + 1) * PC].rearrange("c p w s -> p c w s"), [], b_tb)
                for cc in range(2):
                    pa, b_pa = acc[cc]
                    mms = []
                    for ci in range(PC):
                        kc = pc * PC + ci
                        first = (kc == 0)
                        lastk = (kc == NCH - 1)
                        mms.append(lambda kc=kc, ci=ci, cc=cc, pa=pa, tb=tb, first=first: nc.tensor.matmul(
                            pa[:], aball[:, kc, 128 * cc:128 * cc + 128], tb[:, ci, 0, :], start=first, stop=False))
                        mms.append(lambda kc=kc, ci=ci, cc=cc, pa=pa, tb=tb, lastk=lastk: nc.tensor.matmul(
                            pa[:], aball[:, kc, 256 + 128 * cc:256 + 128 * cc + 128], tb[:, ci, 1, :], start=False, stop=lastk))
                    k.mm(mms, [b_abg[(pc * PC) // 8], b_tb], [b_pa])
            f_t, b_f = fT.next()
            for cc in range(2):
                k.op(k.act if cc == 0 else k.dve,
                     (lambda cc=cc, f_t=f_t: nc.scalar.copy(f_t[:, cc, :], acc[cc][0][:])) if cc == 0 else
                     (lambda cc=cc, f_t=f_t: nc.vector.tensor_copy(f_t[:, cc, :], acc[cc][0][:])),
                     [acc[cc][1]], [b_f])
            y_t, b_y = yf.next()
            for oc in range(2):
                py, b_py = psy.next()
                k.mm([(lambda cc=cc, oc=oc, py=py, f_t=f_t: nc.tensor.matmul(py[:], wf[:, cc, 128 * oc:128 * oc + 128], f_t[:, cc, :],
                                                                            start=(cc == 0), stop=(cc == 1))) for cc in range(2)],
                     [b_wf, b_f], [b_py])
                k.op(k.act, lambda oc=oc, py=py, y_t=y_t: nc.scalar.copy(y_t[:, oc, :], py[:]), [b_py], [b_y])
            k.dma(k.sp, self.CAT[0:2, :, 512 * t:512 * t + 512].rearrange("c p s -> p c s"), y_t[:], [b_y], bCAT, nowaw=True)
        self.end_phase(st)

    def tile_type(self, t):
        if self.NT == 1:
            return 0
        if t == 0:
            return 0
        if t == self.NT - 1:
            return self.ntype - 1
        return 1

    def phase_pool(self, l):
        nc, k, S, NCH = self.nc, self.k, self.S, self.NCH
        self.phase = f"m3b{l}"
        st, T, P = self.scope()
        upall, b_upall = T([128, NCH, 256], BF16, "upall")
        for c0 in range(0, NCH, 8):
            c1 = min(NCH, c0 + 8)
            k.dma(k.sp, upall[:, c0:c1, :], self.UPd[128 * c0:128 * c1, :].rearrange("(c p) f -> p c f", p=128), [], b_upall, nowaw=True)
        band, b_band = T([128, self.ntype, 4 * 6 * 512], BF16, "band")
        k.dma(k.sp, band[:], self.band_d.rearrange("ty p g c s -> p ty (g c s)"), [], b_band)
        wpl, b_wpl = T([64, 4, 128], BF16, "wpl")
        wst, b_wst = T([64, 4, 64], F32, "wst")
        k.dma(k.sp, wst[:], self.w_pool[l].rearrange("g c d -> c g d"), [], b_wst)
        k.op(k.dve, lambda: nc.vector.memset(wpl[:], 0.0), [], [b_wpl])
        for g in range(4):
            k.op(k.dve, lambda g=g: nc.vector.tensor_copy(wpl[:, g, 64 * (g % 2):64 * (g % 2) + 64], wst[:, g, :]), [b_wst], [b_wpl])
        pooled = Ring([T([64, 4, 512], BF16, "pooled") for _ in range(2)])
        yp = Ring([T([128, 2, 512], BF16, "yp") for _ in range(2)])
        psp = Ring([P([128, 512], F32, "psp") for _ in range(4)])
        psy = Ring([P([128, 512], F32, "psy") for _ in range(2)])
        bCAT = self.dbuf("CAT")
        for t in range(self.NT):
            ty = self.tile_type(t)
            pl, b_pl = pooled.next()
            for g in range(4):
                pp, b_pp = psp.next()
                chunks = [(ci, 4 * t - 1 + ci) for ci in range(6) if 0 <= 4 * t - 1 + ci < NCH]
                mms = []
                for n_, (ci, kc) in enumerate(chunks):
                    o = (g * 6 + ci) * 512
                    mms.append(lambda kc=kc, g=g, o=o, pp=pp, ty=ty, st_=(n_ == 0), sp_=(n_ == len(chunks) - 1): nc.tensor.matmul(
                        pp[0:64, :], upall[:, kc, 64 * g:64 * g + 64], band[:, ty, o:o + 512], start=st_, stop=sp_))
                k.mm(mms, [b_upall, b_band], [b_pp])
                k.op(k.act if g % 2 == 0 else k.dve,
                     (lambda g=g, pp=pp, pl=pl: nc.scalar.copy(pl[:, g, :], pp[0:64, :])) if g % 2 == 0 else
                     (lambda g=g, pp=pp, pl=pl: nc.vector.tensor_copy(pl[:, g, :], pp[0:64, :])),
                     [b_pp], [b_pl])
            y_t, b_y = yp.next()
            for oc in range(2):
                py, b_py = psy.next()
                k.mm([(lambda g=g, oc=oc, py=py, pl=pl: nc.tensor.matmul(py[:], wpl[:, g, :], pl[:, g, :],
                                                                        start=(g == 2 * oc), stop=(g == 2 * oc + 1))) for g in (2 * oc, 2 * oc + 1)],
                     [b_wpl, b_pl], [b_py])
                k.op(k.act, lambda oc=oc, py=py, y_t=y_t: nc.scalar.activation(y_t[:, oc, :], py[:], AF.Identity, scale=self.pscale[:, l, oc:oc + 1]),
                     [b_py, self.b_pscale], [b_y])
            k.dma(k.sp, self.CAT[6:8, :, 512 * t:512 * t + 512].rearrange("c p s -> p c s"), y_t[:], [b_y], bCAT, nowaw=True)
        self.end_phase(st)

    def phase_attn(self, l):
        nc, k, S, NCH = self.nc, self.k, self.S, self.NCH
        self.phase = f"m2{l}"
        lam_init = 0.8 - 0.6 * math.exp(-0.3 * l)
        st, T, P = self.scope()
        ktall, b_kt = T([128, 4, S], BF16, "ktall")
        vsb, b_v = T([128, NCH, 512], BF16, "vsb")
        b_kth = [Buf(f"kt{h}") for h in range(4)]
        b_vg = [Buf(f"vg{g}") for g in range((NCH + 7) // 8)]
        k.dma(k.sp, ktall[:, 0, :], self.KT[0], [], b_kth[0])
        k.dma(k.sp, vsb[:, 0:min(NCH, 8), :], self.Vd[0:128 * min(NCH, 8), :].rearrange("(c p) f -> p c f", p=128), [], b_vg[0])
        for h in range(1, 4):
            k.dma(k.sp, ktall[:, h, :], self.KT[h], [], b_kth[h])
        for g in range(1, len(b_vg)):
            c0, c1 = 8 * g, min(NCH, 8 * g + 8)
            k.dma(k.sp, vsb[:, c0:c1, :], self.Vd[128 * c0:128 * c1, :].rearrange("(c p) f -> p c f", p=128), [], b_vg[g])
        ones_bf, b_obf = T([128, 128], BF16, "ones_bf")
        ones_f, b_of = T([128, 128], F32, "ones_f")
        k.op(k.pool, lambda: nc.gpsimd.memset(ones_bf[:], 1.0), [], [b_obf])
        k.op(k.pool, lambda: nc.gpsimd.memset(ones_f[:], 1.0), [], [b_of])
        epst, b_eps = T([128, 1], F32, "epst")
        k.op(k.pool, lambda: nc.gpsimd.memset(epst[:], SUBLN_EPS), [], [b_eps])
        gs0, b_gs0 = T([128, 1], F32, "gs0")
        gsf, b_gsf = T([128, 1], F32, "gsf")
        k.dma(k.sp, gs0[:], self.g_sub[l].rearrange("(p o) -> p o", o=1), [], b_gs0)
        k.op(k.dve, lambda: nc.vector.tensor_scalar(gsf[:], gs0[:], 1.0 - lam_init, None, ALU.mult), [b_gs0], [b_gsf])
        qts = Ring([T([128, 4, 2, 512], BF16, "qt") for _ in range(2)])
        for q_t, b_q in qts.items:
            k.op(k.pool, lambda q_t=q_t: nc.gpsimd.memset(q_t[:], 0.0), [], [b_q])
        pts = Ring([T([128, 512], BF16, "pT") for _ in range(4)])
        rl = [Ring([T([128, 512], F32, f"rl{m}") for _ in range(1)]) for m in range(2)]
        av = Ring([T([128, 512], F32, "av") for _ in range(1)])
        bv = Ring([T([128, 512], F32, "bv") for _ in range(1)])
        ov = Ring([T([128, 512], F32, "ov") for _ in range(1)])
        sqv = Ring([T([128, 512], F32, "sqv") for _ in range(1)])
        msv = Ring([T([128, 512], F32, "msv") for _ in range(1)])
        ydT = Ring([T([128, 512], BF16, "ydT") for _ in range(3)])
        pss = Ring([P([128, 512], F32, "pss") for _ in range(3)])
        psO = [P([128, 512], F32, "psO") for _ in range(2)]
        psL = [P([128, 512], F32, "psL") for _ in range(2)]
        psN = Ring([P([128, 512], F32, "psN") for _ in range(1)])
        bCAT = self.dbuf("CAT")
        LA = 2
        last = NCH - 1

        ocs = [Ring([T([128, 512], F32, f"oc{m}") for _ in range(1)]) for m in range(2)]
        held = {}

        def evac(m):
            oc_t, b_oc = ocs[m].next()
            r_t, b_r = rl[m].next()
            k.op(k.dve, lambda: nc.vector.tensor_copy(oc_t[:], psO[m][0][:]), [psO[m][1]], [b_oc])
            k.op(k.dve, lambda: nc.vector.reciprocal(r_t[:], psL[m][0][:]), [psL[m][1]], [b_r])
            held[m] = (oc_t, b_oc, r_t, b_r)

        def finish(t, h):
            a_t, b_a = av.next()
            b_t, b_b = bv.next()
            o_t, b_o = ov.next()
            oc0, b_oc0, r0, b_r0 = held[0]
            oc1, b_oc1, r1, b_r1 = held[1]
            k.op(k.dve, lambda: nc.vector.tensor_tensor(a_t[:], oc0[:], r0[:], ALU.mult), [b_oc0, b_r0], [b_a])
            k.op(k.dve, lambda: nc.vector.tensor_tensor(b_t[:], oc1[:], r1[:], ALU.mult), [b_oc1, b_r1], [b_b])
            k.op(k.dve, lambda: nc.vector.scalar_tensor_tensor(o_t[:], b_t[:], self.lam_bc[:, l:l + 1], a_t[:], ALU.mult, ALU.add),
                 [b_a, b_b, self.b_lam], [b_o])
            sq_t, b_sq = sqv.next()
            pn, b_pn = psN.next()
            ms_t, b_ms = msv.next()
            y_t, b_y = ydT.next()

            def s0():
                k.op(k.act, lambda: nc.scalar.activation(sq_t[:], o_t[:], AF.Square), [b_o], [b_sq])

            def s1():
                k.mm([lambda: nc.tensor.matmul(pn[:], ones_f[:], sq_t[:], start=True, stop=True)], [b_of, b_sq], [b_pn])

            def s2():
                k.op(k.act, lambda: nc.scalar.activation(ms_t[:], pn[:], AF.Ln, bias=epst[:, 0:1], scale=1.0 / 128), [b_pn, b_eps], [b_ms])
                k.op(k.act, lambda: nc.scalar.activation(ms_t[:], ms_t[:], AF.Exp, scale=-0.5), [b_ms], [b_ms])

            def s3():
                k.op(k.dve, lambda: nc.vector.scalar_tensor_tensor(y_t[:], o_t[:], gsf[:, 0:1], ms_t[:], ALU.mult, ALU.mult),
                     [b_o, b_gsf, b_ms], [b_y])
                k.dma(k.sp, self.CAT[2 + h, :, 512 * t:512 * t + 512], y_t[:], [b_y], bCAT, nowaw=True)
            for dly, fn in ((10, s0), (12, s1), (14, s2), (16, s3)):
                deferred.append([dly, fn])

        deferred = []

        def tick():
            for d in deferred:
                d[0] -= 1
            while deferred and deferred[0][0] <= 0:
                deferred.pop(0)[1]()

        for t in range(self.NT):
            q_t, b_q = qts.next()
            for m in range(2):
                k.dma(k.sp, q_t[64 * m:64 * m + 64, :, m, :],
                      self.QT[:, 64 * m:64 * m + 64, 512 * t:512 * t + 512].rearrange("c p s -> p c s"), [], b_q, nowaw=(m == 1))
            steps = [(h, m, kc) for h in range(4) for m in range(2) for kc in range(NCH)]
            pend = {}

            def emit_st(i):
                h, m, kc = steps[i]
                ps, b_ps = pss.next()
                k.mm([lambda: nc.tensor.matmul(
                    ps[:], ktall[:, h, 128 * kc:128 * kc + 128], q_t[:, h, m, :], start=True, stop=True)],
                    [b_kth[h], b_q], [b_ps])
                p_t, b_p = pts.next()
                k.op(k.act, lambda: nc.scalar.activation(p_t[:], ps[:], AF.Exp, scale=0.125), [b_ps], [b_p])
                pend[i] = (p_t, b_p)

            def emit_pv(i):
                h, m, kc = steps[i]
                p_t, b_p = pend.pop(i)
                k.mm([lambda: nc.tensor.matmul(psO[m][0][:], vsb[:, kc, 128 * h:128 * h + 128], p_t[:], start=(kc == 0), stop=(kc == last)),
                      lambda: nc.tensor.matmul(psL[m][0][:], ones_bf[:], p_t[:], start=(kc == 0), stop=(kc == last))],
                     [b_p, b_vg[kc // 8], b_obf], [psO[m][1], psL[m][1]])
                if kc == last:
                    evac(m)
                    if m == 1:
                        finish(t, h)

            n = len(steps)
            for i in range(n + LA):
                if i < n:
                    emit_st(i)
                if i - LA >= 0:
                    emit_pv(i - LA)
                tick()
        while deferred:
            deferred.pop(0)[1]()
        self.end_phase(st)

    def load_m4(self, l, T, W):
        k = self.k
        wo, b_wo = T([128, 8, D], BF16, "wo")
        k.dma(k.pool, wo[:], self.w_out[l].rearrange("(c p) d -> p c d", p=128), [], b_wo)
        gg, b_gg = T([128, D], F32, "gg4")
        k.dma(k.sp, gg[:], self.GG[l * 3 + 1], [], b_gg)
        W["wo"], W["b_wo"], W["gg4"], W["b_gg4"] = wo, b_wo, gg, b_gg

    def phase_m4(self, l, W=None):
        nc, k, S = self.nc, self.k, self.S
        self.phase = f"m4{l}"
        st, T, P = self.scope()
        if W is None or "wo" not in W:
            W = {} if W is None else W
            self.load_m4(l, T, W)
        wo, b_wo, gg, b_gg = W["wo"], W["b_wo"], W["gg4"], W["b_gg4"]
        R = {"junk": Ring([T([128, 512], BF16, "junk")]), "ss": Ring([T([128, 4], F32, "ss") for _ in range(4)]),
             "tmp": Ring([T([128, D], F32, "tmp") for _ in range(1)])}
        cat = Ring([T([128, 8, 512], BF16, "cat") for _ in range(2)])
        xpost = Ring([T([128, D], F32, "xpost") for _ in range(6)])
        psy = Ring([P([128, 512], F32, "psy") for _ in range(4)])
        b_dst = self.dbuf("xs_out")

        def loads(t):
            c_t, b_c = cat.next()
            k.dma(k.sp, c_t[:], self.CAT[:, :, 512 * t:512 * t + 512].rearrange("c p s -> p c s"), [], b_c)
            xs_ = []
            for sub in range(4):
                x_t, b_x = xpost.next()
                r0 = 512 * t + 128 * sub
                k.dma(k.sp, x_t[:], self.xs[r0:r0 + 128, :], [], b_x)
                xs_.append((x_t, b_x))
            return c_t, b_c, xs_
        nxt = loads(0)
        for t in range(self.NT):
            c_t, b_c, xs_ = nxt
            for sub in range(4):
                if sub == 2 and t + 1 < self.NT:
                    nxt = loads(t + 1)
                yh = []
                for h in range(2):
                    py, b_py = psy.next()
                    k.mm([(lambda c=c, py=py, sub=sub, h=h, c_t=c_t: nc.tensor.matmul(
                        py[:], c_t[:, c, 128 * sub:128 * sub + 128], wo[:, c, 512 * h:512 * h + 512],
                        start=(c == 0), stop=(c == 7))) for c in range(8)],
                        [b_c, b_wo], [b_py])
                    yh.append((py[:], b_py))
                x_t, b_x = xs_[sub]
                r0 = 512 * t + 128 * sub
                self.post_sub(yh, x_t[:], b_x, gg, b_gg, R, self.xs[r0:r0 + 128, :], b_dst)
        self.end_phase(st)


def _bf(a):
    return np.ascontiguousarray(a.astype(np.float32)).astype(ml_dtypes.bfloat16)


def make_constants(S):
    NT, NCH = S // 512, S // 128
    c = {}
    c["ident"] = _bf(np.eye(128))
    j = np.arange(64)
    ang = 2 * np.pi * ((j[:, None] * j[None, :]) % 64) / 64.0
    scale = 1.0 / math.sqrt(S * 64.0)
    wcs = np.zeros((256, 512), np.float64)
    for h in range(4):
        wcs[64 * h:64 * h + 64, 64 * h:64 * h + 64] = np.cos(ang) * scale
        wcs[64 * h:64 * h + 64, 256 + 64 * h:256 + 64 * h + 64] = -np.sin(ang) * scale
    c["wcs"] = _bf(wcs)
    s_in = np.arange(S, dtype=np.int64)
    dft = np.empty((NT, NCH, 128, 2, 512), dtype=ml_dtypes.bfloat16)
    for t in range(NT):
        s_out = np.arange(512 * t, 512 * t + 512, dtype=np.int64)
        a = (2 * np.pi / S) * ((s_in[:, None] * s_out[None, :]) % S).astype(np.float64)
        dft[t, :, :, 0, :] = _bf(np.cos(a)).reshape(NCH, 128, 512)
        dft[t, :, :, 1, :] = _bf(np.sin(a)).reshape(NCH, 128, 512)
    c["dft"] = dft
    ntype = 1 if NT == 1 else (2 if NT == 2 else 3)
    reps = [0] if NT == 1 else ([0, NT - 1] if NT == 2 else [0, 1, NT - 1])
    band = np.zeros((ntype, 128, 4, 6, 512), np.float64)
    for ti, t in enumerate(reps):
        s_out = np.arange(512 * t, 512 * t + 512)
        for g, w in enumerate((2, 4, 8, 16)):
            half = w // 2
            count = np.minimum(s_out + half, S) - np.maximum(s_out - half, 0)
            for ci in range(6):
                kc = 4 * t - 1 + ci
                if kc < 0 or kc >= NCH:
                    continue
                si = 128 * kc + np.arange(128)
                inwin = (si[:, None] >= s_out[None, :] - half) & (si[:, None] <= s_out[None, :] + half - 1)
                m = inwin / count[None, :] - (si[:, None] == s_out[None, :])
                band[ti, :, g, ci, :] = m
    c["band"] = _bf(band)
    inv = 1.0 / (10000.0 ** (np.arange(0, 64, 2, dtype=np.float32) / np.float32(64)))
    c["invf"] = np.ascontiguousarray(np.tile(inv.astype(np.float32), 4).reshape(128, 1))
    return c


def make_in_maps(inputs, S, ncores):
    f32 = lambda a: np.ascontiguousarray(np.asarray(a, dtype=np.float32))
    x = f32(inputs["x"])
    c = f32(inputs["c"])
    consts = make_constants(S)
    shared = dict(consts)
    shared["pos"] = np.ascontiguousarray(np.broadcast_to(np.asarray(inputs["positions"], dtype=np.int32)[None, :], (128, S)))
    shared["w_ada"] = f32(inputs["w_ada"])
    b_ada = f32(inputs["b_ada"])
    shared["b_ada"] = b_ada
    shared["b_ada_fm"] = np.ascontiguousarray(b_ada.reshape(DEPTH, 9, 8, 128).transpose(3, 0, 1, 2).reshape(128, DEPTH, 72))
    shared["g_pre_fm"] = np.ascontiguousarray(f32(inputs["g_pre"]).reshape(DEPTH * 3, 8, 128).transpose(2, 0, 1))
    shared["g_post"] = f32(inputs["g_post"]).reshape(DEPTH * 3, D)
    shared["w_ff_gu"] = f32(inputs["w_ff_gu"])
    shared["w_ff_down"] = f32(inputs["w_ff_down"])
    shared["w_in"] = f32(inputs["w_in"])
    shared["w_fourier"] = f32(inputs["w_fourier"])
    shared["lamv"] = np.ascontiguousarray(np.concatenate(
        [f32(inputs[n]) for n in ("lambda_q1", "lambda_k1", "lambda_q2", "lambda_k2")], axis=1))
    shared["g_subln"] = f32(inputs["g_subln"])
    shared["w_pool"] = f32(inputs["w_pool"])
    shared["pscale_fm"] = np.ascontiguousarray(f32(inputs["pool_scale"]).reshape(DEPTH, 2, 128).transpose(2, 0, 1))
    shared["w_out"] = f32(inputs["w_out"])
    maps = []
    for b in range(ncores):
        m = dict(shared)
        m["x"] = np.ascontiguousarray(x[b])
        m["c_fm"] = np.ascontiguousarray(c[b].reshape(8, 128).T)
        maps.append(m)
    return maps


_PROG_CACHE = {}


def kernel(**inputs):
    x = np.asarray(inputs["x"])
    B, S, _ = x.shape
    if S not in _PROG_CACHE:
        _PROG_CACHE[S] = Prog(S).build()
    nc = _PROG_CACHE[S]
    maps = make_in_maps(inputs, S, B)
    res = run_bass_kernel_spmd(nc, maps, core_ids=list(range(B)))
    out = np.stack([np.asarray(r["out"], dtype=np.float32) for r in res.results], axis=0)
    return out
```

```python
import math
from contextlib import ExitStack
import numpy as np
import ml_dtypes
import concourse.bass as bass
import concourse.mybir as mybir
from concourse.bass_utils import run_bass_kernel_spmd

F32 = mybir.dt.float32
BF16 = mybir.dt.bfloat16
I32 = mybir.dt.int32
AF = mybir.ActivationFunctionType
ALU = mybir.AluOpType
AX = mybir.AxisListType

D = 1024
DFF = 2816
NJ = DFF // 128
DEPTH = 2
SEQ = 4096
NCORES = 8
NORM_EPS = 1e-6
SUBLN_EPS = 1e-5
TWO_PI = 2.0 * math.pi
CW1 = 6.28125
CW2 = TWO_PI - CW1


class Buf:
    __slots__ = ("name", "w", "r", "slot", "psum", "sslot")

    def __init__(self, name, psum=False):
        self.name = name
        self.w = None
        self.r = []
        self.slot = None
        self.sslot = None
        self.psum = psum


class Eng:
    def __init__(self, name, h, sem):
        self.name, self.h, self.sem = name, h, sem
        self.cnt = 0
        self.waited = {}


class K:
    def __init__(self, nc, stack):
        self.nc = nc
        self.stack = stack
        self.nsem = 0
        mk = lambda n, h: Eng(n, h, self._new_sem("e_" + n))
        self.pe = mk("pe", nc.tensor)
        self.act = mk("act", nc.scalar)
        self.dve = mk("dve", nc.vector)
        self.pool = mk("pool", nc.gpsimd)
        self.sp = mk("sp", nc.sync)
        self.engines = [self.pe, self.act, self.dve, self.pool, self.sp]
        self.free_slots = []
        self.live_slots = []
        self.ninst = 0

    def _new_sem(self, name):
        self.nsem += 1
        return self.stack.enter_context(self.nc.semaphore(f"{name}_{self.nsem}"))

    def _need(self, E, reads, writes):
        need = {}

        def add(ev, same_ok):
            if ev is None:
                return
            sem, val = ev
            if same_ok and sem is E.sem:
                return
            kk = id(sem)
            if kk not in need or need[kk][1] < val:
                need[kk] = (sem, val)
        for b in reads:
            add(b.w, False)
            if b.psum:
                for ev in b.r:
                    add(ev, True)
        for b in writes:
            add(b.w, True)
            for ev in b.r:
                add(ev, True)
        for kk, (sem, val) in need.items():
            if E.waited.get(kk, 0) < val:
                E.h.wait_ge(sem, val)
                E.waited[kk] = val

    def _done(self, ev, reads, writes):
        for b in reads:
            b.r.append(ev)
            if len(b.r) > 48:
                d = {}
                for s, v in b.r:
                    if id(s) not in d or d[id(s)][1] < v:
                        d[id(s)] = (s, v)
                b.r = list(d.values())
        for b in writes:
            b.w = ev
            b.r = []

    def op(self, E, fn, reads=(), writes=()):
        self._need(E, reads, writes)
        ins = fn()
        E.cnt += 1
        ins.then_inc(E.sem, 1)
        self._done((E.sem, E.cnt), reads, writes)
        self.ninst += 1

    def mm(self, mms, reads, writes):
        E = self.pe
        self._need(E, reads, writes)
        ins = None
        for fn in mms:
            ins = fn()
            self.ninst += 1
        E.cnt += 1
        ins.then_inc(E.sem, 1)
        self._done((E.sem, E.cnt), reads, writes)

    def dma(self, Q, out, in_, reads, dst, nowaw=False, **kw):
        self._need(Q, reads, [] if nowaw else [dst])
        own = reads[0] if (nowaw and len(reads) > 0) else None
        if own is not None:
            if own.sslot is None:
                own.sslot = self.free_slots.pop() if self.free_slots else [self._new_sem("s"), 0]
                self.live_slots.append(own.sslot)
            slot = own.sslot
        else:
            if dst.slot is None:
                dst.slot = self.free_slots.pop() if self.free_slots else [self._new_sem("d"), 0]
                self.live_slots.append(dst.slot)
            slot = dst.slot
        ins = Q.h.dma_start(out=out, in_=in_, **kw)
        slot[1] += 16
        ins.then_inc(slot[0], 16)
        ev = (slot[0], slot[1])
        self._done(ev, reads, [])
        dst.w = ev
        if not nowaw:
            dst.r = []
        self.ninst += 1

    def barrier(self):
        evs = [(e.sem, e.cnt) for e in self.engines if e.cnt > 0]
        evs += [(s[0], s[1]) for s in self.live_slots if s[1] > 0]
        for E in self.engines:
            for sem, val in evs:
                if sem is E.sem:
                    continue
                if E.waited.get(id(sem), 0) < val:
                    E.h.wait_ge(sem, val)
                    E.waited[id(sem)] = val
        self.free_slots.extend(self.live_slots)
        self.live_slots = []


class Ring:
    def __init__(self, items):
        self.items = items
        self.i = 0

    def next(self):
        it = self.items[self.i % len(self.items)]
        self.i += 1
        return it


class Prog:
    def __init__(self, S, dbg=False):
        self.S = S
        self.NT = S // 512
        self.NCH = S // 128
        self.dbg = dbg
        nc = self.nc = bass.Bass("TRN2", target_bir_lowering=False)
        din = lambda n, s, dt=F32: nc.dram_tensor(n, list(s), dt, kind="ExternalInput").ap()
        dsc = lambda n, s, dt=F32: nc.dram_tensor(n, list(s), dt, kind=("ExternalOutput" if dbg else "Internal")).ap()
        self.x_d = din("x", [S, D])
        self.c_fm = din("c_fm", [128, 8])
        self.pos_d = din("pos", [128, S], I32)
        self.inv_d = din("invf", [128, 1])
        self.w_ada = din("w_ada", [DEPTH, D, 9 * D])
        self.b_ada = din("b_ada", [DEPTH, 9 * D])
        self.b_ada_fm = din("b_ada_fm", [128, DEPTH, 72])
        self.g_pre_fm = din("g_pre_fm", [128, DEPTH * 3, 8])
        self.g_post = din("g_post", [DEPTH * 3, D])
        self.w_gu = din("w_ff_gu", [DEPTH, 2, D, 2 * DFF])
        self.w_dn = din("w_ff_down", [DEPTH, 2, DFF, D])
        self.w_in = din("w_in", [DEPTH, D, 2048])
        self.w_f = din("w_fourier", [DEPTH, 256, 256])
        self.lamv = din("lamv", [DEPTH, 4 * 64])
        self.g_sub = din("g_subln", [DEPTH, 128])
        self.w_pool = din("w_pool", [DEPTH, 4, 64, 64])
        self.pscale_fm = din("pscale_fm", [128, DEPTH, 2])
        self.w_out = din("w_out", [DEPTH, D, D])
        self.ident_d = din("ident", [128, 128], BF16)
        self.wcs_d = din("wcs", [256, 512], BF16)
        self.dft_d = din("dft", [self.NT, self.NCH, 128, 2, 512], BF16)
        self.ntype = 1 if self.NT == 1 else (2 if self.NT == 2 else 3)
        self.band_d = din("band", [self.ntype, 128, 4, 6, 512], BF16)
        self.out_d = nc.dram_tensor("out", [S, D], F32, kind="ExternalOutput").ap()
        self.xs = dsc("xs", [S, D])
        self.GG = dsc("GG", [DEPTH * 3, 128, D])
        self.CS = dsc("CS", [2, 128, S])
        self.QT = dsc("QT", [4, 128, S], BF16)
        self.KT = dsc("KT", [4, 128, S], BF16)
        self.Vd = dsc("Vd", [S, 512], BF16)
        self.ABd = dsc("ABd", [S, 512], BF16)
        self.UPd = dsc("UPd", [S, 256], BF16)
        self.CAT = dsc("CAT", [8, 128, S], BF16)
        self.dram_bufs = {}

    def dbuf(self, name):
        if name not in self.dram_bufs:
            self.dram_bufs[name] = Buf("dram_" + name)
        b = self.dram_bufs[name]
        return b

    def scope(self):
        st = ExitStack()
        nc = self.nc
        cnt = [0]

        def T(shape, dt, name=None):
            cnt[0] += 1
            t = st.enter_context(nc.sbuf_tensor(f"{name or 't'}_{self.phase}_{cnt[0]}", list(shape), dt))
            return t, Buf(f"{name}_{cnt[0]}")

        def P(shape, dt, name=None):
            cnt[0] += 1
            t = st.enter_context(nc.psum_tensor(f"{name or 'p'}_{self.phase}_{cnt[0]}", list(shape), dt))
            return t, Buf(f"ps{name}_{cnt[0]}", psum=True)
        return st, T, P

    def end_phase(self, st):
        self.k.barrier()
        st.close()
        for b in self.dram_bufs.values():
            b.slot = None
            b.w = None
            b.r = []

    def build(self, layers=DEPTH, phases=None):
        nc = self.nc
        with ExitStack() as top:
            self.k = k = K(nc, top)
            self.phase = "top"
            cnt = [0]

            def TT(shape, dt, name):
                cnt[0] += 1
                return top.enter_context(nc.sbuf_tensor(f"g_{name}", list(shape), dt)), Buf(name)
            self.ident, self.b_ident = TT([128, 128], BF16, "ident")
            self.mhalf, self.b_mhalf = TT([128, 1], F32, "mhalf")
            self.ones_row, self.b_ones = TT([1, 128], F32, "ones_row")
            self.modT, self.b_modT = TT([128, DEPTH * 3 * 2, 8], F32, "modT")
            self.lam_bc, self.b_lam = TT([128, DEPTH], F32, "lam_bc")
            self.gsub_bc, self.b_gsub = TT([128, DEPTH, 128], F32, "gsub_bc")
            self.pscale, self.b_pscale = TT([128, DEPTH, 2], F32, "pscale")
            k.dma(k.sp, self.ident[:], self.ident_d[:], [], self.b_ident)
            k.dma(k.sp, self.pscale[:], self.pscale_fm[:], [], self.b_pscale)
            k.op(k.pool, lambda: nc.gpsimd.memset(self.mhalf[:], -0.5), [], [self.b_mhalf])
            k.op(k.pool, lambda: nc.gpsimd.memset(self.ones_row[:], 1.0), [], [self.b_ones])
            want = lambda p: phases is None or p in phases
            def outer_scope():
                stx = ExitStack()
                c = [0]

                def TX(shape, dt, name=None):
                    c[0] += 1
                    self.pfx = getattr(self, "pfx", 0) + 1
                    t = stx.enter_context(nc.sbuf_tensor(f"pf_{name}_{self.pfx}", list(shape), dt))
                    return t, Buf(f"pf_{name}_{self.pfx}")
                return stx, TX
            full = phases is None
            W0 = None
            stA = None
            if full:
                stA, TA = outer_scope()
                W0 = {}
                self.load_wgu(0, 0, TA, W0)
                self.load_wdn(0, 0, TA, W0)
            if want("p0"):
                self.phase_p0()
            for l in range(layers):
                first = (l == 0)
                last = (l == layers - 1)
                if want("f1"):
                    self.phase_ffn(l, 0, src=(self.x_d if first else self.xs), dst=self.xs, W=(W0 if first else None))
                if first and stA is not None:
                    stA.close()
                if want("m1"):
                    self.phase_m1(l)
                if want("m3a"):
                    self.phase_fourier(l)
                if want("m3b"):
                    self.phase_pool(l)
                W2 = None
                stB = None
                if full:
                    stB, TB = outer_scope()
                    W2 = {}
                    self.load_wgu(l, 1, TB, W2)
                if want("m2"):
                    self.phase_attn(l)
                stC = None
                if full:
                    issue_wdn = self.load_wdn(l, 1, TB, W2, defer=True)
                    stC, TC = outer_scope()
                    self.load_m4(l, TC, W2)
                    issue_wdn()
                if want("m4"):
                    self.phase_m4(l, W2)
                if stC is not None:
                    stC.close()
                if want("f2"):
                    self.phase_ffn(l, 1, src=self.xs, dst=(self.out_d if last else self.xs), W=W2)
                if stB is not None:
                    stB.close()
            k.barrier()
        return nc

    def phase_p0(self):
        nc, k, S = self.nc, self.k, self.S
        self.phase = "p0"
        st, T, P = self.scope()
        CHW = min(S, 512)
        posi, b_posi = T([128, CHW], I32, "posi")
        ang, b_ang = T([128, CHW], F32, "ang")
        t1, b_t1 = T([128, CHW], F32, "t1")
        ni, b_ni = T([128, CHW], I32, "ni")
        nf, b_nf = T([128, CHW], F32, "nf")
        invt, b_inv = T([128, 1], F32, "invt")
        b_CS = self.dbuf("CS")
        k.dma(k.act, invt[:], self.inv_d[:], [], b_inv)
        def rope_chunk(c0):
            k.dma(k.act, posi[:], self.pos_d[:, c0:c0 + CHW], [], b_posi)
            k.op(k.dve, lambda: nc.vector.tensor_copy(ang[:], posi[:]), [b_posi], [b_ang])
            k.op(k.dve, lambda: nc.vector.tensor_scalar(ang[:], ang[:], invt[:, 0:1], None, ALU.mult), [b_ang, b_inv], [b_ang])
            for which in range(2):
                off = (math.pi / 2) if which == 0 else 0.0
                k.op(k.dve, lambda off=off: nc.vector.tensor_scalar(t1[:], ang[:], off, 1.0 / TWO_PI, ALU.add, ALU.mult), [b_ang], [b_t1])
                k.op(k.dve, lambda: nc.vector.tensor_copy(ni[:], t1[:]), [b_t1], [b_ni])
                k.op(k.dve, lambda: nc.vector.tensor_copy(nf[:], ni[:]), [b_ni], [b_nf])
                k.op(k.dve, lambda off=off: nc.vector.tensor_scalar(t1[:], ang[:], off, None, ALU.add), [b_ang], [b_t1])
                k.op(k.dve, lambda: nc.vector.scalar_tensor_tensor(t1[:], nf[:], -CW1, t1[:], ALU.mult, ALU.add), [b_nf, b_t1], [b_t1])
                k.op(k.dve, lambda: nc.vector.scalar_tensor_tensor(t1[:], nf[:], -CW2, t1[:], ALU.mult, ALU.add), [b_nf, b_t1], [b_t1])
                k.op(k.dve, lambda: nc.vector.tensor_scalar(t1[:], t1[:], -math.pi, math.pi, ALU.max, ALU.min), [b_t1], [b_t1])
                k.op(k.act, lambda: nc.scalar.activation(nf[:], t1[:], AF.Sin), [b_t1], [b_nf])
                k.dma(k.act, self.CS[which, :, c0:c0 + CHW], nf[:], [b_nf], b_CS, nowaw=True)
        rope_todo = list(range(0, S, CHW))
        cT, b_cT = T([128, 8], F32, "cT")
        cact, b_cact = T([128, 8], F32, "cact")
        crep, b_crep = T([128, 8, 128], F32, "crep")
        bfm, b_bfm = T([128, DEPTH, 72], F32, "bfm")
        gpre, b_gpre = T([128, DEPTH * 3, 8], F32, "gpre")
        k.dma(k.sp, cT[:], self.c_fm[:], [], b_cT)
        k.dma(k.sp, bfm[:], self.b_ada_fm[:], [], b_bfm)
        k.dma(k.sp, gpre[:], self.g_pre_fm[:], [], b_gpre)
        k.op(k.act, lambda: nc.scalar.activation(cact[:], cT[:], AF.Silu), [b_cT], [b_cact])
        for kk in range(8):
            k.op(k.dve, lambda kk=kk: nc.vector.tensor_copy(crep[:, kk, :], cact[:, kk:kk + 1].to_broadcast([128, 128])),
                 [b_cact], [b_crep])
        wp = Ring([T([128, 8, 512], F32, "wp") for _ in range(2)])
        brow = Ring([T([1, 512], F32, "brow") for _ in range(2)])
        grow = Ring([T([1, 512], F32, "grow") for _ in range(2)])
        gpsb = Ring([T([128, 512], F32, "gpsb") for _ in range(2)])
        ggt = Ring([T([128, 512], F32, "ggt") for _ in range(2)])
        tmp8 = Ring([T([128, 8], F32, "tmp8") for _ in range(2)])
        ps_fm = Ring([P([128, 512], F32, "psfm") for _ in range(2)])
        ps_g = Ring([P([128, 512], F32, "psg") for _ in range(2)])
        ps_p = Ring([P([128, 512], F32, "psp") for _ in range(2)])
        b_GG = self.dbuf("GG")
        for l in range(DEPTH):
            for s in range(3):
                for j in range(3):
                    v = s * 3 + j
                    if j < 2:
                        pfm, b_pfm = ps_fm.next()
                    for hh in range(2):
                        w_t, b_w = wp.next()
                        c0 = v * D + 512 * hh
                        k.dma(k.sp, w_t[:], self.w_ada[l, :, c0:c0 + 512].rearrange("(kk p) c -> p kk c", p=128), [], b_w)
                        if rope_todo:
                            rope_chunk(rope_todo.pop(0))
                        if j < 2:
                            for cch in range(4):
                                col = 4 * hh + cch
                                k.mm([(lambda kk=kk, cch=cch, col=col, w_t=w_t, pfm=pfm: nc.tensor.matmul(
                                    pfm[:, col:col + 1], w_t[:, kk, 128 * cch:128 * cch + 128], cact[:, kk:kk + 1],
                                    start=(kk == 0), stop=(kk == 7))) for kk in range(8)],
                                    [b_w, b_cact], [b_pfm])
                        else:
                            br, b_br = brow.next()
                            gr, b_gr = grow.next()
                            k.dma(k.sp, br[:], self.b_ada[l:l + 1, c0:c0 + 512], [], b_br)
                            k.dma(k.sp, gr[:], self.g_post[l * 3 + s:l * 3 + s + 1, 512 * hh:512 * hh + 512], [], b_gr)
                            pg, b_pg = ps_g.next()
                            pp, b_pp = ps_p.next()
                            mms = [(lambda kk=kk, w_t=w_t, pg=pg: nc.tensor.matmul(
                                pg[:], crep[:, kk, :], w_t[:, kk, :], start=(kk == 0), stop=False)) for kk in range(8)]
                            mms.append(lambda br=br, pg=pg: nc.tensor.matmul(pg[:], self.ones_row[0:1, :], br[0:1, :], start=False, stop=True))
                            k.mm(mms, [b_w, b_crep, b_br, self.b_ones], [b_pg])
                            k.mm([lambda gr=gr, pp=pp: nc.tensor.matmul(pp[:], self.ones_row[0:1, :], gr[0:1, :], start=True, stop=True)],
                                 [b_gr, self.b_ones], [b_pp])
                            gs_t, b_gs = gpsb.next()
                            k.op(k.act, lambda gs_t=gs_t, pp=pp: nc.scalar.copy(gs_t[:], pp[:]), [b_pp], [b_gs])
                            gg_t, b_gg = ggt.next()
                            mw = 1.0 if s == 1 else 0.5
                            k.op(k.dve, lambda gg_t=gg_t, pg=pg, gs_t=gs_t, mw=mw: nc.vector.scalar_tensor_tensor(
                                gg_t[:], pg[:], mw, gs_t[:], ALU.mult, ALU.mult), [b_pg, b_gs], [b_gg])
                            k.dma(k.sp, self.GG[l * 3 + s, :, 512 * hh:512 * hh + 512], gg_t[:], [b_gg], b_GG, nowaw=True)
                    if j < 2:
                        idx = (l * 3 + s) * 2
                        if j == 0:
                            k.op(k.dve, lambda pfm=pfm, idx=idx, l=l, v=v: nc.vector.tensor_tensor(
                                self.modT[:, idx + 1, :], pfm[:, 0:8], bfm[:, l, v * 8:v * 8 + 8], ALU.add),
                                [b_pfm, b_bfm], [self.b_modT])
                        else:
                            t8, b_t8 = tmp8.next()
                            k.op(k.dve, lambda pfm=pfm, t8=t8, l=l, v=v: nc.vector.tensor_tensor(
                                t8[:], pfm[:, 0:8], bfm[:, l, v * 8:v * 8 + 8], ALU.add), [b_pfm, b_bfm], [b_t8])
                            k.op(k.dve, lambda t8=t8, idx=idx, l=l, s=s: nc.vector.scalar_tensor_tensor(
                                self.modT[:, idx, :], t8[:], 1.0, gpre[:, l * 3 + s, :], ALU.add, ALU.mult),
                                [b_t8, b_gpre], [self.b_modT])
        while rope_todo:
            rope_chunk(rope_todo.pop(0))
        lamr, b_lamr = T([1, DEPTH, 256], F32, "lamr")
        lprod, b_lprod = T([1, DEPTH, 2, 64], F32, "lprod")
        lsum, b_lsum = T([1, DEPTH, 2], F32, "lsum")
        lexp, b_lexp = T([1, DEPTH, 2], F32, "lexp")
        lval, b_lval = T([1, DEPTH], F32, "lval")
        gsr, b_gsr = T([1, DEPTH, 128], F32, "gsr")
        k.dma(k.sp, lamr[:], self.lamv.rearrange("(o l) f -> o l f", o=1), [], b_lamr)
        k.dma(k.sp, gsr[:], self.g_sub.rearrange("(o l) f -> o l f", o=1), [], b_gsr)
        for l in range(DEPTH):
            lam_init = 0.8 - 0.6 * math.exp(-0.3 * l)
            for m in range(2):
                k.op(k.dve, lambda l=l, m=m: nc.vector.tensor_tensor(
                    lprod[:, l, m, :], lamr[:, l, 128 * m:128 * m + 64], lamr[:, l, 128 * m + 64:128 * m + 128], ALU.mult),
                    [b_lamr], [b_lprod])
            k.op(k.dve, lambda l=l: nc.vector.reduce_sum(lsum[:, l, :], lprod[:, l, :, :], AX.X), [b_lprod], [b_lsum])
            k.op(k.act, lambda l=l: nc.scalar.activation(lexp[:, l, :], lsum[:, l, :], AF.Exp), [b_lsum], [b_lexp])
            k.op(k.dve, lambda l=l, li=lam_init: nc.vector.scalar_tensor_tensor(
                lval[:, l:l + 1], lexp[:, l, 1:2], -li, lexp[:, l, 0:1], ALU.add, ALU.subtract), [b_lexp], [b_lval])
            pp, b_pp = ps_p.next()
            k.mm([lambda l=l, pp=pp: nc.tensor.matmul(pp[:, 0:1], self.ones_row[0:1, :], lval[0:1, l:l + 1], start=True, stop=True)],
                 [b_lval, self.b_ones], [b_pp])
            k.op(k.act, lambda l=l, pp=pp: nc.scalar.copy(self.lam_bc[:, l:l + 1], pp[:, 0:1]), [b_pp], [self.b_lam])
            pg, b_pg = ps_g.next()
            k.mm([lambda l=l, pg=pg: nc.tensor.matmul(pg[:, 0:128], self.ones_row[0:1, :], gsr[0:1, l, :], start=True, stop=True)],
                 [b_gsr, self.b_ones], [b_pg])
            k.op(k.act, lambda l=l, pg=pg, li=lam_init: nc.scalar.mul(self.gsub_bc[:, l, :], pg[:, 0:128], 1.0 - li), [b_pg], [self.b_gsub])
        self.end_phase(st)

    def pre_A(self, x_t, b_x, R):
        nc, k = self.nc, self.k
        junk, b_junk = R["junk"].next()
        ss, b_ss = R["ss"].next()
        xn, b_xn = R["xn"].next()
        k.op(k.act, lambda: nc.scalar.activation(junk[:], x_t, AF.Square, accum_out=ss[:, 0:1]), [b_x], [b_junk, b_ss])
        k.op(k.dve, lambda: nc.vector.tensor_scalar(ss[:, 1:2], ss[:, 0:1], 1.0 / D, NORM_EPS, ALU.mult, ALU.add), [b_ss], [b_ss])
        k.op(k.pool, lambda: nc.gpsimd.tensor_tensor(ss[:, 2:3], ss[:, 1:2], self.mhalf[:], ALU.pow), [b_ss, self.b_mhalf], [b_ss])
        k.op(k.dve, lambda: nc.vector.tensor_scalar(xn[:], x_t, ss[:, 2:3], None, ALU.mult), [b_x, b_ss], [b_xn])
        return xn, b_xn

    def pre_B(self, l, s, xn, b_xn, hT, b_hT, sub, R):
        nc, k = self.nc, self.k
        pt, b_pt = R["pt"].next()
        k.mm([(lambda c=c: nc.tensor.transpose(pt[:, c, :], xn[:, 128 * c:128 * c + 128], self.ident[:])) for c in range(8)],
             [b_xn, self.b_ident], [b_pt])
        idx = (l * 3 + s) * 2
        for c in range(8):
            if sub % 2 == 0:
                k.op(k.act, lambda c=c: nc.scalar.activation(
                    hT[:, c, 128 * sub:128 * sub + 128], pt[:, c, :], AF.Identity,
                    bias=self.modT[:, idx + 1, c:c + 1], scale=self.modT[:, idx, c:c + 1]), [b_pt, self.b_modT], [b_hT])
            else:
                k.op(k.dve, lambda c=c: nc.vector.tensor_scalar(
                    hT[:, c, 128 * sub:128 * sub + 128], pt[:, c, :], self.modT[:, idx, c:c + 1], self.modT[:, idx + 1, c:c + 1],
                    ALU.mult, ALU.add), [b_pt, self.b_modT], [b_hT])

    def pre_sub(self, l, s, x_t, b_x, hT, b_hT, sub, R):
        xn, b_xn = self.pre_A(x_t, b_x, R)
        self.pre_B(l, s, xn, b_xn, hT, b_hT, sub, R)

    def pre_sched(self, l, s, src, t, hTs, xpre, R):
        k = self.k
        hT, b_hT = hTs.next()
        xns = {}

        def A(sub):
            x_t, b_x = xpre.next()
            r0 = 512 * t + 128 * sub
            k.dma(k.sp, x_t[:], src[r0:r0 + 128, :], [], b_x)
            xns[sub] = self.pre_A(x_t[:], b_x, R)

        def B(sub):
            xn, b_xn = xns[sub]
            self.pre_B(l, s, xn, b_xn, hT, b_hT, sub, R)
        st1 = lambda: (A(0), A(1))
        st2 = lambda: (B(0), B(1), A(2), A(3))
        st3 = lambda: (B(2), B(3))
        return hT, b_hT, [st1, st2, st3]

    def pre_rings(self, T, P):
        return {
            "junk": Ring([T([128, D], BF16, "junk") for _ in range(1)]),
            "ss": Ring([T([128, 4], F32, "ss") for _ in range(4)]),
            "xn": Ring([T([128, D], BF16, "xn") for _ in range(2)]),
            "pt": Ring([P([128, 8, 128], BF16, "pt") for _ in range(1)]),
        }

    def post_sub(self, yh, x_t, b_x, gg, b_gg, R, dst_rows, b_dst):
        nc, k = self.nc, self.k
        import os
        pstop = int(os.environ.get("POST_STOP", "99"))
        junk, b_junk = R["junk"].next()
        ss, b_ss = R["ss"].next()
        tmp, b_tmp = R["tmp"].next()
        if pstop < 1:
            return
        for h in range(2):
            k.op(k.act, lambda h=h: nc.scalar.activation(junk[:, 0:512], yh[h][0], AF.Square, accum_out=ss[:, h:h + 1]),
                 [yh[h][1]], [b_junk, b_ss])
            k.op(k.dve, lambda h=h: nc.vector.tensor_tensor(tmp[:, 512 * h:512 * h + 512], yh[h][0], gg[:, 512 * h:512 * h + 512], ALU.mult),
                 [yh[h][1], b_gg], [b_tmp])
        if pstop < 2:
            return
        k.op(k.dve, lambda: nc.vector.tensor_tensor(ss[:, 2:3], ss[:, 0:1], ss[:, 1:2], ALU.add), [b_ss], [b_ss])
        k.op(k.dve, lambda: nc.vector.tensor_scalar(ss[:, 3:4], ss[:, 2:3], 1.0 / D, NORM_EPS, ALU.mult, ALU.add), [b_ss], [b_ss])
        k.op(k.pool, lambda: nc.gpsimd.tensor_tensor(ss[:, 2:3], ss[:, 3:4], self.mhalf[:], ALU.pow), [b_ss, self.b_mhalf], [b_ss])
        if pstop < 3:
            return
        k.op(k.dve, lambda: nc.vector.scalar_tensor_tensor(x_t, tmp[:], ss[:, 2:3], x_t, ALU.mult, ALU.add), [b_tmp, b_ss, b_x], [b_x])
        if pstop < 4:
            return
        k.dma(k.sp, dst_rows, x_t, [b_x], b_dst, nowaw=True)

    def load_wgu(self, l, f, T, W):
        k = self.k
        wgu, _ = T([128, 8, 2 * DFF], BF16, "wgu")
        groups = [(g0, min(NJ, g0 + 4)) for g0 in range(0, NJ, 4)]
        b_wg = []
        for gi, (g0, g1) in enumerate(groups):
            bg, bu = Buf(f"wg{gi}"), Buf(f"wu{gi}")
            for base, bb in ((0, bg), (DFF, bu)):
                c0, c1 = base + 128 * g0, base + 128 * g1
                k.dma(k.pool, wgu[:, :, c0:c1], self.w_gu[l, f, :, c0:c1].rearrange("(kk p) c -> p kk c", p=128), [], bb)
            b_wg.append((bg, bu))
        W["wgu"], W["b_wg"] = wgu, b_wg

    def load_wdn(self, l, f, T, W, defer=False):
        k = self.k
        wdn, _ = T([128, NJ, D], BF16, "wdn")
        b_wdnh = [Buf(f"wdn{h}") for h in range(2)]

        def issue():
            for h in range(2):
                k.dma(k.pool, wdn[:, 11 * h:11 * h + 11, :],
                      self.w_dn[l, f, 1408 * h:1408 * h + 1408, :].rearrange("(j p) d -> p j d", p=128), [], b_wdnh[h])
        W["wdn"], W["b_wdnh"] = wdn, b_wdnh
        if defer:
            return issue
        issue()

    def phase_ffn(self, l, f, src, dst, W=None):
        nc, k, S = self.nc, self.k, self.S
        s = 0 if f == 0 else 2
        self.phase = f"ffn{l}{f}"
        st, T, P = self.scope()
        late_w = W is None
        gg, b_gg = T([128, D], F32, "gg")
        k.dma(k.sp, gg[:], self.GG[l * 3 + s], [], b_gg)
        R = self.pre_rings(T, P)
        R["tmp"] = Ring([T([128, D], F32, "tmp") for _ in range(1)])
        xpre = Ring([T([128, D], F32, "xpre") for _ in range(2)])
        xpost = Ring([T([128, D], F32, "xpost") for _ in range(2)])
        hTs = Ring([T([128, 8, 512], BF16, "hT") for _ in range(2)])
        actT, b_act = T([128, NJ, 512], BF16, "actT")
        sg = Ring([T([128, 512], F32, "sg") for _ in range(2)])
        psg = Ring([P([128, 512], F32, "psg") for _ in range(2)])
        psu = Ring([P([128, 512], F32, "psu") for _ in range(2)])
        psy = Ring([P([128, 512], F32, "psy") for _ in range(3)])
        b_dst = self.dbuf("xs_out")

        def do_pre(t):
            hT, b_hT = hTs.next()
            for sub in range(4):
                x_t, b_x = xpre.next()
                r0 = 512 * t + 128 * sub
                k.dma(k.sp, x_t[:], src[r0:r0 + 128, :], [], b_x)
                self.pre_sub(l, s, x_t[:], b_x, hT, b_hT, sub, R)
            return hT, b_hT

        nxt = do_pre(0)
        if late_w:
            W = {}
            self.load_wgu(l, f, T, W)
            self.load_wdn(l, f, T, W)
        wgu, b_wg, wdn, b_wdnh = W["wgu"], W["b_wg"], W["wdn"], W["b_wdnh"]
        stop = 99
        for t in range(self.NT):
            hT, b_hT = nxt
            for j in range(NJ):
                pg, b_pg = psg.next()
                pu, b_pu = psu.next()
                k.mm([(lambda kk=kk, j=j, pg=pg: nc.tensor.matmul(pg[:], wgu[:, kk, 128 * j:128 * j + 128], hT[:, kk, :],
                                                                  start=(kk == 0), stop=(kk == 7))) for kk in range(8)],
                     [b_hT, b_wg[j // 4][0]], [b_pg])
                k.mm([(lambda kk=kk, j=j, pu=pu: nc.tensor.matmul(pu[:], wgu[:, kk, DFF + 128 * j:DFF + 128 * j + 128], hT[:, kk, :],
                                                                  start=(kk == 0), stop=(kk == 7))) for kk in range(8)],
                     [b_hT, b_wg[j // 4][1]], [b_pu])
                sg_t, b_sg = sg.next()
                k.op(k.act, lambda pg=pg, sg_t=sg_t: nc.scalar.activation(sg_t[:], pg[:], AF.Silu), [b_pg], [b_sg])
                k.op(k.dve, lambda j=j, pu=pu, sg_t=sg_t: nc.vector.tensor_tensor(actT[:, j, :], pu[:], sg_t[:], ALU.mult),
                     [b_pu, b_sg], [b_act])
                if t + 1 < self.NT:
                    if j == 3:
                        nh, nb, stg = self.pre_sched(l, s, src, t + 1, hTs, xpre, R)
                        nxt = (nh, nb)
                        stg[0]()
                    elif j == 9:
                        stg[1]()
                    elif j == 15:
                        stg[2]()
            for sub in range(4 if stop >= 3 else 0):
                yh = []
                for h in range(2):
                    py, b_py = psy.next()
                    k.mm([(lambda j=j, py=py, sub=sub, h=h: nc.tensor.matmul(
                        py[:], actT[:, j, 128 * sub:128 * sub + 128], wdn[:, j, 512 * h:512 * h + 512],
                        start=(j == 0), stop=(j == NJ - 1))) for j in range(NJ)],
                        [b_act] + b_wdnh, [b_py])
                    yh.append((py[:], b_py))
                x_t, b_x = xpost.next()
                r0 = 512 * t + 128 * sub
                k.dma(k.sp, x_t[:], src[r0:r0 + 128, :], [], b_x)
                self.post_sub(yh, x_t[:], b_x, gg, b_gg, R, dst[r0:r0 + 128, :], b_dst)
        self.end_phase(st)

    def phase_m1(self, l):
        nc, k, S = self.nc, self.k, self.S
        self.phase = f"m1{l}"
        st, T, P = self.scope()
        win, b_win = T([128, 8, 2048], BF16, "win")
        wrot, b_wrot = T([128, 8, 1024], BF16, "wrot")
        wcs, b_wcs = T([128, 2, 512], BF16, "wcs")
        b_win_a, b_win_b, b_win_c = Buf("win_a"), Buf("win_b"), Buf("win_c")
        for (c0, c1, bb) in ((0, 256, b_win_a), (256, 1280, b_win_b), (1280, 2048, b_win_c)):
            k.dma(k.pool, win[:, :, c0:c1], self.w_in[l][:, c0:c1].rearrange("(kk p) c -> p kk c", p=128), [], bb)
        k.dma(k.sp, wcs[:], self.wcs_d.rearrange("(c p) f -> p c f", p=128), [], b_wcs)
        for kk in range(8):
            src = win[:, kk, 256:1280].rearrange("p (m t d) -> p m t d", t=2, d=32)
            dstv = wrot[:, kk, :].rearrange("p (m t d) -> p m t d", t=2, d=32)
            k.op(k.dve, lambda src=src, dstv=dstv: nc.vector.tensor_scalar(dstv[:, :, 0, :], src[:, :, 1, :], -1.0, None, ALU.mult),
                 [b_win_b], [b_wrot])
            k.op(k.dve, lambda src=src, dstv=dstv: nc.vector.tensor_copy(dstv[:, :, 1, :], src[:, :, 0, :]), [b_win_b], [b_wrot])
        R = self.pre_rings(T, P)
        xpre = Ring([T([128, D], F32, "xpre") for _ in range(2)])
        hTs = Ring([T([128, 8, 512], BF16, "hT") for _ in range(2)])
        cs = Ring([T([128, 2, 512], F32, "cs") for _ in range(2)])
        ufT = Ring([T([128, 2, 512], BF16, "ufT") for _ in range(2)])
        ab = Ring([T([128, 4, 512], BF16, "ab") for _ in range(2)])
        qk = Ring([T([128, 8, 512], BF16, "qk") for _ in range(2)])
        vv = Ring([T([128, 4, 512], BF16, "vv") for _ in range(2)])
        up = Ring([T([128, 4, 256], BF16, "up") for _ in range(2)])
        r1 = Ring([T([128, 512], F32, "r1") for _ in range(2)])
        r2 = Ring([T([128, 512], F32, "r2") for _ in range(2)])
        psa = Ring([P([128, 512], F32, "psa") for _ in range(3)])
        psb = Ring([P([128, 512], F32, "psb") for _ in range(3)])
        bQT, bKT, bV, bAB, bUP = (self.dbuf(n) for n in ("QT", "KT", "V", "AB", "UP"))
        src_x = self.xs
        nh, nb, stg = self.pre_sched(l, 1, src_x, 0, hTs, xpre, R)
        for f_ in stg:
            f_()
        nxt = (nh, nb)
        for t in range(self.NT):
            hT, b_hT = nxt
            stg = None
            if t + 1 < self.NT:
                nh, nb, stg = self.pre_sched(l, 1, src_x, t + 1, hTs, xpre, R)
                nxt = (nh, nb)
                stg[0]()
            cs_t, b_cs = cs.next()
            k.dma(k.sp, cs_t[:], self.CS[:, :, 512 * t:512 * t + 512].rearrange("w p s -> p w s"), [], b_cs)
            uf, b_uf = ufT.next()
            for cc in range(2):
                pa, b_pa = psa.next()
                k.mm([(lambda kk=kk, cc=cc, pa=pa: nc.tensor.matmul(pa[:], win[:, kk, 128 * cc:128 * cc + 128], hT[:, kk, :],
                                                                    start=(kk == 0), stop=(kk == 7))) for kk in range(8)],
                     [b_win_a, b_hT], [b_pa])
                k.op(k.act, lambda cc=cc, pa=pa, uf=uf: nc.scalar.copy(uf[:, cc, :], pa[:]), [b_pa], [b_uf])
            ab_t, b_ab = ab.next()
            for sub in range(4):
                pb, b_pb = psb.next()
                k.mm([(lambda cc=cc, sub=sub, pb=pb, uf=uf: nc.tensor.matmul(pb[:], uf[:, cc, 128 * sub:128 * sub + 128], wcs[:, cc, :],
                                                                            start=(cc == 0), stop=(cc == 1))) for cc in range(2)],
                     [b_uf, b_wcs], [b_pb])
                k.op(k.act, lambda sub=sub, pb=pb, ab_t=ab_t: nc.scalar.copy(ab_t[:, sub, :], pb[:]), [b_pb], [b_ab])
            k.dma(k.sp, self.ABd[512 * t:512 * t + 512, :].rearrange("(s p) f -> p s f", p=128), ab_t[:], [b_ab], bAB, nowaw=True)
            qk_t, b_qk = qk.next()
            for c in range(8):
                if stg is not None and c == 0:
                    stg[1]()
                if stg is not None and c == 4:
                    stg[2]()
                pa, b_pa = psa.next()
                pb, b_pb = psb.next()
                k.mm([(lambda kk=kk, c=c, pa=pa: nc.tensor.matmul(pa[:], win[:, kk, 256 + 128 * c:256 + 128 * c + 128], hT[:, kk, :],
                                                                  start=(kk == 0), stop=(kk == 7))) for kk in range(8)],
                     [b_win_b, b_hT], [b_pa])
                k.mm([(lambda kk=kk, c=c, pb=pb: nc.tensor.matmul(pb[:], wrot[:, kk, 128 * c:128 * c + 128], hT[:, kk, :],
                                                                  start=(kk == 0), stop=(kk == 7))) for kk in range(8)],
                     [b_wrot, b_hT], [b_pb])
                a1, b_a1 = r1.next()
                a2, b_a2 = r2.next()
                k.op(k.dve, lambda pa=pa, a1=a1, cs_t=cs_t: nc.vector.tensor_tensor(a1[:], pa[:], cs_t[:, 0, :], ALU.mult), [b_pa, b_cs], [b_a1])
                k.op(k.dve, lambda pb=pb, a2=a2, cs_t=cs_t: nc.vector.tensor_tensor(a2[:], pb[:], cs_t[:, 1, :], ALU.mult), [b_pb, b_cs], [b_a2])
                k.op(k.pool, lambda c=c, a1=a1, a2=a2, qk_t=qk_t: nc.gpsimd.tensor_tensor(qk_t[:, c, :], a1[:], a2[:], ALU.add), [b_a1, b_a2], [b_qk])
            k.dma(k.sp, self.QT[:, :, 512 * t:512 * t + 512].rearrange("c p s -> p c s"), qk_t[:, 0:4, :], [b_qk], bQT, nowaw=True)
            k.dma(k.sp, self.KT[:, :, 512 * t:512 * t + 512].rearrange("c p s -> p c s"), qk_t[:, 4:8, :], [b_qk], bKT, nowaw=True)
            v_t, b_v = vv.next()
            u_t, b_u = up.next()
            for sub in range(4):
                pa, b_pa = psa.next()
                pb, b_pb = psb.next()
                k.mm([(lambda kk=kk, sub=sub, pa=pa: nc.tensor.matmul(pa[:], hT[:, kk, 128 * sub:128 * sub + 128], win[:, kk, 1280:1792],
                                                                      start=(kk == 0), stop=(kk == 7))) for kk in range(8)],
                     [b_win_c, b_hT], [b_pa])
                k.mm([(lambda kk=kk, sub=sub, pb=pb: nc.tensor.matmul(pb[:, 0:256], hT[:, kk, 128 * sub:128 * sub + 128], win[:, kk, 1792:2048],
                                                                      start=(kk == 0), stop=(kk == 7))) for kk in range(8)],
                     [b_win_c, b_hT], [b_pb])
                k.op(k.act, lambda sub=sub, pa=pa, v_t=v_t: nc.scalar.copy(v_t[:, sub, :], pa[:]), [b_pa], [b_v])
                k.op(k.dve, lambda sub=sub, pb=pb, u_t=u_t: nc.vector.tensor_copy(u_t[:, sub, :], pb[:, 0:256]), [b_pb], [b_u])
            k.dma(k.sp, self.Vd[512 * t:512 * t + 512, :].rearrange("(s p) f -> p s f", p=128), v_t[:], [b_v], bV, nowaw=True)
            k.dma(k.sp, self.UPd[512 * t:512 * t + 512, :].rearrange("(s p) f -> p s f", p=128), u_t[:], [b_u], bUP, nowaw=True)
        self.end_phase(st)

    def phase_fourier(self, l):
        nc, k, S, NCH = self.nc, self.k, self.S, self.NCH
        self.phase = f"m3a{l}"
        st, T, P = self.scope()
        aball, b_aball = T([128, NCH, 512], BF16, "aball")
        b_abg = [Buf(f"abg{g}") for g in range((NCH + 7) // 8)]

        def load_ab(g):
            c0, c1 = 8 * g, min(NCH, 8 * g + 8)
            k.dma(k.sp, aball[:, c0:c1, :], self.ABd[128 * c0:128 * c1, :].rearrange("(c p) f -> p c f", p=128), [], b_abg[g])
        load_ab(0)
        ab_rest = list(range(1, len(b_abg)))
        wf, b_wf = T([128, 2, 256], BF16, "wf")
        k.dma(k.pool, wf[:], self.w_f[l].rearrange("(c p) f -> p c f", p=128), [], b_wf)
        PC = 4 if NCH >= 4 else NCH
        tab = Ring([T([128, PC, 2, 512], BF16, "tab") for _ in range(3)])
        fT = Ring([T([128, 2, 512], BF16, "fT") for _ in range(2)])
        yf = Ring([T([128, 2, 512], BF16, "yf") for _ in range(2)])
        psf = Ring([P([128, 512], F32, "psf") for _ in range(4)])
        psy = Ring([P([128, 512], F32, "psy") for _ in range(2)])
        bCAT = self.dbuf("CAT")
        for t in range(self.NT):
            acc = [psf.next() for _ in range(2)]
            npc = NCH // PC
            for pc in range(npc):
                tb, b_tb = tab.next()
                k.dma(k.sp, tb[:], self.dft_d[t, pc * PC:(pc + 1) * PC].rearrange("c p w s -> p c w s"), [], b_tb)
                if ab_rest and pc < 2:
                    while ab_rest:
                        load_ab(ab_rest.pop(0))
                for cc in range(2):
                    pa, b_pa = acc[cc]
                    mms = []
                    for ci in range(PC):
                        kc = pc * PC + ci
                        first = (kc == 0)
                        lastk = (kc == NCH - 1)
                        mms.append(lambda kc=kc, ci=ci, cc=cc, pa=pa, tb=tb, first=first: nc.tensor.matmul(
                            pa[:], aball[:, kc, 128 * cc:128 * cc + 128], tb[:, ci, 0, :], start=first, stop=False))
                        mms.append(lambda kc=kc, ci=ci, cc=cc, pa=pa, tb=tb, lastk=lastk: nc.tensor.matmul(
                            pa[:], aball[:, kc, 256 + 128 * cc:256 + 128 * cc + 128], tb[:, ci, 1, :], start=False, stop=lastk))
                    k.mm(mms, [b_abg[(pc * PC) // 8], b_tb], [b_pa])
            f_t, b_f = fT.next()
            for cc in range(2):
                k.op(k.act if cc == 0 else k.dve,
                     (lambda cc=cc, f_t=f_t: nc.scalar.copy(f_t[:, cc, :], acc[cc][0][:])) if cc == 0 else
                     (lambda cc=cc, f_t=f_t: nc.vector.tensor_copy(f_t[:, cc, :], acc[cc][0][:])),
                     [acc[cc][1]], [b_f])
            y_t, b_y = yf.next()
            for oc in range(2):
                py, b_py = psy.next()
                k.mm([(lambda cc=cc, oc=oc, py=py, f_t=f_t: nc.tensor.matmul(py[:], wf[:, cc, 128 * oc:128 * oc + 128], f_t[:, cc, :],
                                                                            start=(cc == 0), stop=(cc == 1))) for cc in range(2)],
                     [b_wf, b_f], [b_py])
                k.op(k.act, lambda oc=oc, py=py, y_t=y_t: nc.scalar.copy(y_t[:, oc, :], py[:]), [b_py], [b_y])
            k.dma(k.sp, self.CAT[0:2, :, 512 * t:512 * t + 512].rearrange("c p s -> p c s"), y_t[:], [b_y], bCAT, nowaw=True)
        self.end_phase(st)

    def tile_type(self, t):
        if self.NT == 1:
            return 0
        if t == 0:
            return 0
        if t == self.NT - 1:
            return self.ntype - 1
        return 1

    def phase_pool(self, l):
        nc, k, S, NCH = self.nc, self.k, self.S, self.NCH
        self.phase = f"m3b{l}"
        st, T, P = self.scope()
        upall, b_upall = T([128, NCH, 256], BF16, "upall")
        for c0 in range(0, NCH, 8):
            c1 = min(NCH, c0 + 8)
            k.dma(k.sp, upall[:, c0:c1, :], self.UPd[128 * c0:128 * c1, :].rearrange("(c p) f -> p c f", p=128), [], b_upall, nowaw=True)
        band, b_band = T([128, self.ntype, 4 * 6 * 512], BF16, "band")
        k.dma(k.sp, band[:], self.band_d.rearrange("ty p g c s -> p ty (g c s)"), [], b_band)
        wpl, b_wpl = T([64, 4, 128], BF16, "wpl")
        wst, b_wst = T([64, 4, 64], F32, "wst")
        k.dma(k.sp, wst[:], self.w_pool[l].rearrange("g c d -> c g d"), [], b_wst)
        k.op(k.dve, lambda: nc.vector.memset(wpl[:], 0.0), [], [b_wpl])
        for g in range(4):
            k.op(k.dve, lambda g=g: nc.vector.tensor_copy(wpl[:, g, 64 * (g % 2):64 * (g % 2) + 64], wst[:, g, :]), [b_wst], [b_wpl])
        pooled = Ring([T([64, 4, 512], BF16, "pooled") for _ in range(2)])
        yp = Ring([T([128, 2, 512], BF16, "yp") for _ in range(2)])
        psp = Ring([P([128, 512], F32, "psp") for _ in range(4)])
        psy = Ring([P([128, 512], F32, "psy") for _ in range(2)])
        bCAT = self.dbuf("CAT")
        for t in range(self.NT):
            ty = self.tile_type(t)
            pl, b_pl = pooled.next()
            for g in range(4):
                pp, b_pp = psp.next()
                chunks = [(ci, 4 * t - 1 + ci) for ci in range(6) if 0 <= 4 * t - 1 + ci < NCH]
                mms = []
                for n_, (ci, kc) in enumerate(chunks):
                    o = (g * 6 + ci) * 512
                    mms.append(lambda kc=kc, g=g, o=o, pp=pp, ty=ty, st_=(n_ == 0), sp_=(n_ == len(chunks) - 1): nc.tensor.matmul(
                        pp[0:64, :], upall[:, kc, 64 * g:64 * g + 64], band[:, ty, o:o + 512], start=st_, stop=sp_))
                k.mm(mms, [b_upall, b_band], [b_pp])
                k.op(k.act if g % 2 == 0 else k.dve,
                     (lambda g=g, pp=pp, pl=pl: nc.scalar.copy(pl[:, g, :], pp[0:64, :])) if g % 2 == 0 else
                     (lambda g=g, pp=pp, pl=pl: nc.vector.tensor_copy(pl[:, g, :], pp[0:64, :])),
                     [b_pp], [b_pl])
            y_t, b_y = yp.next()
            for oc in range(2):
                py, b_py = psy.next()
                k.mm([(lambda g=g, oc=oc, py=py, pl=pl: nc.tensor.matmul(py[:], wpl[:, g, :], pl[:, g, :],
                                                                        start=(g == 2 * oc), stop=(g == 2 * oc + 1))) for g in (2 * oc, 2 * oc + 1)],
                     [b_wpl, b_pl], [b_py])
                k.op(k.act, lambda oc=oc, py=py, y_t=y_t: nc.scalar.activation(y_t[:, oc, :], py[:], AF.Identity, scale=self.pscale[:, l, oc:oc + 1]),
                     [b_py, self.b_pscale], [b_y])
            k.dma(k.sp, self.CAT[6:8, :, 512 * t:512 * t + 512].rearrange("c p s -> p c s"), y_t[:], [b_y], bCAT, nowaw=True)
        self.end_phase(st)

    def phase_attn(self, l):
        nc, k, S, NCH = self.nc, self.k, self.S, self.NCH
        self.phase = f"m2{l}"
        lam_init = 0.8 - 0.6 * math.exp(-0.3 * l)
        st, T, P = self.scope()
        ktall, b_kt = T([128, 4, S], BF16, "ktall")
        vsb, b_v = T([128, NCH, 512], BF16, "vsb")
        b_kth = [Buf(f"kt{h}") for h in range(4)]
        b_vg = [Buf(f"vg{g}") for g in range((NCH + 7) // 8)]

        def load_kv_first():
            k.dma(k.sp, ktall[:, 0, :], self.KT[0], [], b_kth[0])
            k.dma(k.sp, vsb[:, 0:min(NCH, 8), :], self.Vd[0:128 * min(NCH, 8), :].rearrange("(c p) f -> p c f", p=128), [], b_vg[0])

        def load_kv_rest():
            for h in range(1, 4):
                k.dma(k.sp, ktall[:, h, :], self.KT[h], [], b_kth[h])
            for g in range(1, len(b_vg)):
                c0, c1 = 8 * g, min(NCH, 8 * g + 8)
                k.dma(k.sp, vsb[:, c0:c1, :], self.Vd[128 * c0:128 * c1, :].rearrange("(c p) f -> p c f", p=128), [], b_vg[g])
        ones_bf, b_obf = T([128, 128], BF16, "ones_bf")
        ones_f, b_of = T([128, 128], F32, "ones_f")
        k.op(k.dve, lambda: nc.vector.memset(ones_bf[:], 1.0), [], [b_obf])
        k.op(k.dve, lambda: nc.vector.memset(ones_f[:], 1.0), [], [b_of])
        epst, b_eps = T([128, 1], F32, "epst")
        k.op(k.dve, lambda: nc.vector.memset(epst[:], SUBLN_EPS), [], [b_eps])
        gs0, b_gs0 = T([128, 1], F32, "gs0")
        gsf, b_gsf = T([128, 1], F32, "gsf")
        k.dma(k.sp, gs0[:], self.g_sub[l].rearrange("(p o) -> p o", o=1), [], b_gs0)
        k.op(k.dve, lambda: nc.vector.tensor_scalar(gsf[:], gs0[:], 1.0 - lam_init, None, ALU.mult), [b_gs0], [b_gsf])
        qts = Ring([T([128, 4, 2, 512], BF16, "qt") for _ in range(2)])
        for q_t, b_q in qts.items:
            k.op(k.dve, lambda q_t=q_t: nc.vector.memset(q_t[:], 0.0), [], [b_q])
        pts = Ring([T([128, 512], BF16, "pT") for _ in range(4)])
        rl = [Ring([T([128, 512], F32, f"rl{m}") for _ in range(1)]) for m in range(2)]
        av = Ring([T([128, 512], F32, "av") for _ in range(1)])
        bv = Ring([T([128, 512], F32, "bv") for _ in range(1)])
        ov = Ring([T([128, 512], F32, "ov") for _ in range(1)])
        sqv = Ring([T([128, 512], F32, "sqv") for _ in range(1)])
        msv = Ring([T([128, 512], F32, "msv") for _ in range(1)])
        ydT = Ring([T([128, 512], BF16, "ydT") for _ in range(3)])
        pss = Ring([P([128, 512], F32, "pss") for _ in range(3)])
        psO = [P([128, 512], F32, "psO") for _ in range(2)]
        psL = [P([128, 512], F32, "psL") for _ in range(2)]
        psN = Ring([P([128, 512], F32, "psN") for _ in range(1)])
        bCAT = self.dbuf("CAT")
        LA = 2
        last = NCH - 1

        ocs = [Ring([T([128, 512], F32, f"oc{m}") for _ in range(1)]) for m in range(2)]
        held = {}

        def evac(m):
            oc_t, b_oc = ocs[m].next()
            r_t, b_r = rl[m].next()
            k.op(k.dve, lambda: nc.vector.tensor_copy(oc_t[:], psO[m][0][:]), [psO[m][1]], [b_oc])
            k.op(k.dve, lambda: nc.vector.reciprocal(r_t[:], psL[m][0][:]), [psL[m][1]], [b_r])
            held[m] = (oc_t, b_oc, r_t, b_r)

        def finish(t, h):
            a_t, b_a = av.next()
            b_t, b_b = bv.next()
            o_t, b_o = ov.next()
            oc0, b_oc0, r0, b_r0 = held[0]
            oc1, b_oc1, r1, b_r1 = held[1]
            k.op(k.dve, lambda: nc.vector.tensor_tensor(a_t[:], oc0[:], r0[:], ALU.mult), [b_oc0, b_r0], [b_a])
            k.op(k.dve, lambda: nc.vector.tensor_tensor(b_t[:], oc1[:], r1[:], ALU.mult), [b_oc1, b_r1], [b_b])
            k.op(k.dve, lambda: nc.vector.scalar_tensor_tensor(o_t[:], b_t[:], self.lam_bc[:, l:l + 1], a_t[:], ALU.mult, ALU.add),
                 [b_a, b_b, self.b_lam], [b_o])
            sq_t, b_sq = sqv.next()
            pn, b_pn = psN.next()
            ms_t, b_ms = msv.next()
            y_t, b_y = ydT.next()

            def s0():
                k.op(k.act, lambda: nc.scalar.activation(sq_t[:], o_t[:], AF.Square), [b_o], [b_sq])

            def s1():
                k.mm([lambda: nc.tensor.matmul(pn[:], ones_f[:], sq_t[:], start=True, stop=True)], [b_of, b_sq], [b_pn])

            def s2():
                k.op(k.act, lambda: nc.scalar.activation(ms_t[:], pn[:], AF.Ln, bias=epst[:, 0:1], scale=1.0 / 128), [b_pn, b_eps], [b_ms])
                k.op(k.act, lambda: nc.scalar.activation(ms_t[:], ms_t[:], AF.Exp, scale=-0.5), [b_ms], [b_ms])

            def s3():
                k.op(k.dve, lambda: nc.vector.scalar_tensor_tensor(y_t[:], o_t[:], gsf[:, 0:1], ms_t[:], ALU.mult, ALU.mult),
                     [b_o, b_gsf, b_ms], [b_y])
                k.dma(k.sp, self.CAT[2 + h, :, 512 * t:512 * t + 512], y_t[:], [b_y], bCAT, nowaw=True)
            for dly, fn in ((10, s0), (12, s1), (14, s2), (16, s3)):
                deferred.append([dly, fn])

        deferred = []

        def tick():
            for d in deferred:
                d[0] -= 1
            while deferred and deferred[0][0] <= 0:
                deferred.pop(0)[1]()

        def load_q(t):
            q_t, b_q = qts.next()
            for m in range(2):
                k.dma(k.sp, q_t[64 * m:64 * m + 64, :, m, :],
                      self.QT[:, 64 * m:64 * m + 64, 512 * t:512 * t + 512].rearrange("c p s -> p c s"), [], b_q, nowaw=(m == 1))
            return q_t, b_q
        nxt_q = load_q(0)
        load_kv_first()
        load_kv_rest()
        for t in range(self.NT):
            q_t, b_q = nxt_q
            if t + 1 < self.NT:
                nxt_q = load_q(t + 1)
            steps = [(h, m, kc) for h in range(4) for m in range(2) for kc in range(NCH)]
            pend = {}

            def emit_st(i):
                h, m, kc = steps[i]
                ps, b_ps = pss.next()
                k.mm([lambda: nc.tensor.matmul(
                    ps[:], ktall[:, h, 128 * kc:128 * kc + 128], q_t[:, h, m, :], start=True, stop=True)],
                    [b_kth[h], b_q], [b_ps])
                p_t, b_p = pts.next()
                k.op(k.act, lambda: nc.scalar.activation(p_t[:], ps[:], AF.Exp, scale=0.125), [b_ps], [b_p])
                pend[i] = (p_t, b_p)

            def emit_pv(i):
                h, m, kc = steps[i]
                p_t, b_p = pend.pop(i)
                k.mm([lambda: nc.tensor.matmul(psO[m][0][:], vsb[:, kc, 128 * h:128 * h + 128], p_t[:], start=(kc == 0), stop=(kc == last)),
                      lambda: nc.tensor.matmul(psL[m][0][:], ones_bf[:], p_t[:], start=(kc == 0), stop=(kc == last))],
                     [b_p, b_vg[kc // 8], b_obf], [psO[m][1], psL[m][1]])
                if kc == last:
                    evac(m)
                    if m == 1:
                        finish(t, h)

            n = len(steps)
            for i in range(n + LA):
                if i < n:
                    emit_st(i)
                if i - LA >= 0:
                    emit_pv(i - LA)
                tick()
        while deferred:
            deferred.pop(0)[1]()
        self.end_phase(st)

    def load_m4(self, l, T, W):
        k = self.k
        wo, b_wo = T([128, 8, D], BF16, "wo")
        k.dma(k.pool, wo[:], self.w_out[l].rearrange("(c p) d -> p c d", p=128), [], b_wo)
        gg, b_gg = T([128, D], F32, "gg4")
        k.dma(k.sp, gg[:], self.GG[l * 3 + 1], [], b_gg)
        W["wo"], W["b_wo"], W["gg4"], W["b_gg4"] = wo, b_wo, gg, b_gg

    def phase_m4(self, l, W=None):
        nc, k, S = self.nc, self.k, self.S
        self.phase = f"m4{l}"
        st, T, P = self.scope()
        if W is None or "wo" not in W:
            W = {} if W is None else W
            self.load_m4(l, T, W)
        wo, b_wo, gg, b_gg = W["wo"], W["b_wo"], W["gg4"], W["b_gg4"]
        R = {"junk": Ring([T([128, 512], BF16, "junk")]), "ss": Ring([T([128, 4], F32, "ss") for _ in range(4)]),
             "tmp": Ring([T([128, D], F32, "tmp") for _ in range(1)])}
        cat = Ring([T([128, 8, 512], BF16, "cat") for _ in range(2)])
        xpost = Ring([T([128, D], F32, "xpost") for _ in range(6)])
        psy = Ring([P([128, 512], F32, "psy") for _ in range(4)])
        b_dst = self.dbuf("xs_out")

        def loads(t):
            c_t, b_c = cat.next()
            k.dma(k.sp, c_t[:], self.CAT[:, :, 512 * t:512 * t + 512].rearrange("c p s -> p c s"), [], b_c)
            xs_ = []
            for sub in range(4):
                x_t, b_x = xpost.next()
                r0 = 512 * t + 128 * sub
                k.dma(k.sp, x_t[:], self.xs[r0:r0 + 128, :], [], b_x)
                xs_.append((x_t, b_x))
            return c_t, b_c, xs_
        nxt = loads(0)
        for t in range(self.NT):
            c_t, b_c, xs_ = nxt
            for sub in range(4):
                if sub == 2 and t + 1 < self.NT:
                    nxt = loads(t + 1)
                yh = []
                for h in range(2):
                    py, b_py = psy.next()
                    k.mm([(lambda c=c, py=py, sub=sub, h=h, c_t=c_t: nc.tensor.matmul(
                        py[:], c_t[:, c, 128 * sub:128 * sub + 128], wo[:, c, 512 * h:512 * h + 512],
                        start=(c == 0), stop=(c == 7))) for c in range(8)],
                        [b_c, b_wo], [b_py])
                    yh.append((py[:], b_py))
                x_t, b_x = xs_[sub]
                r0 = 512 * t + 128 * sub
                self.post_sub(yh, x_t[:], b_x, gg, b_gg, R, self.xs[r0:r0 + 128, :], b_dst)
        self.end_phase(st)


def _bf(a):
    return np.ascontiguousarray(a.astype(np.float32)).astype(ml_dtypes.bfloat16)


def make_constants(S):
    NT, NCH = S // 512, S // 128
    c = {}
    c["ident"] = _bf(np.eye(128))
    j = np.arange(64)
    ang = 2 * np.pi * ((j[:, None] * j[None, :]) % 64) / 64.0
    scale = 1.0 / math.sqrt(S * 64.0)
    wcs = np.zeros((256, 512), np.float64)
    for h in range(4):
        wcs[64 * h:64 * h + 64, 64 * h:64 * h + 64] = np.cos(ang) * scale
        wcs[64 * h:64 * h + 64, 256 + 64 * h:256 + 64 * h + 64] = -np.sin(ang) * scale
    c["wcs"] = _bf(wcs)
    s_in = np.arange(S, dtype=np.int64)
    dft = np.empty((NT, NCH, 128, 2, 512), dtype=ml_dtypes.bfloat16)
    for t in range(NT):
        s_out = np.arange(512 * t, 512 * t + 512, dtype=np.int64)
        a = (2 * np.pi / S) * ((s_in[:, None] * s_out[None, :]) % S).astype(np.float64)
        dft[t, :, :, 0, :] = _bf(np.cos(a)).reshape(NCH, 128, 512)
        dft[t, :, :, 1, :] = _bf(np.sin(a)).reshape(NCH, 128, 512)
    c["dft"] = dft
    ntype = 1 if NT == 1 else (2 if NT == 2 else 3)
    reps = [0] if NT == 1 else ([0, NT - 1] if NT == 2 else [0, 1, NT - 1])
    band = np.zeros((ntype, 128, 4, 6, 512), np.float64)
    for ti, t in enumerate(reps):
        s_out = np.arange(512 * t, 512 * t + 512)
        for g, w in enumerate((2, 4, 8, 16)):
            half = w // 2
            count = np.minimum(s_out + half, S) - np.maximum(s_out - half, 0)
            for ci in range(6):
                kc = 4 * t - 1 + ci
                if kc < 0 or kc >= NCH:
                    continue
                si = 128 * kc + np.arange(128)
                inwin = (si[:, None] >= s_out[None, :] - half) & (si[:, None] <= s_out[None, :] + half - 1)
                m = inwin / count[None, :] - (si[:, None] == s_out[None, :])
                band[ti, :, g, ci, :] = m
    c["band"] = _bf(band)
    inv = 1.0 / (10000.0 ** (np.arange(0, 64, 2, dtype=np.float32) / np.float32(64)))
    c["invf"] = np.ascontiguousarray(np.tile(inv.astype(np.float32), 4).reshape(128, 1))
    return c


def make_in_maps(inputs, S, ncores):
    f32 = lambda a: np.ascontiguousarray(np.asarray(a, dtype=np.float32))
    x = f32(inputs["x"])
    c = f32(inputs["c"])
    consts = make_constants(S)
    shared = dict(consts)
    shared["pos"] = np.ascontiguousarray(np.broadcast_to(np.asarray(inputs["positions"], dtype=np.int32)[None, :], (128, S)))
    shared["w_ada"] = f32(inputs["w_ada"])
    b_ada = f32(inputs["b_ada"])
    shared["b_ada"] = b_ada
    shared["b_ada_fm"] = np.ascontiguousarray(b_ada.reshape(DEPTH, 9, 8, 128).transpose(3, 0, 1, 2).reshape(128, DEPTH, 72))
    shared["g_pre_fm"] = np.ascontiguousarray(f32(inputs["g_pre"]).reshape(DEPTH * 3, 8, 128).transpose(2, 0, 1))
    shared["g_post"] = f32(inputs["g_post"]).reshape(DEPTH * 3, D)
    shared["w_ff_gu"] = f32(inputs["w_ff_gu"])
    shared["w_ff_down"] = f32(inputs["w_ff_down"])
    shared["w_in"] = f32(inputs["w_in"])
    shared["w_fourier"] = f32(inputs["w_fourier"])
    shared["lamv"] = np.ascontiguousarray(np.concatenate(
        [f32(inputs[n]) for n in ("lambda_q1", "lambda_k1", "lambda_q2", "lambda_k2")], axis=1))
    shared["g_subln"] = f32(inputs["g_subln"])
    shared["w_pool"] = f32(inputs["w_pool"])
    shared["pscale_fm"] = np.ascontiguousarray(f32(inputs["pool_scale"]).reshape(DEPTH, 2, 128).transpose(2, 0, 1))
    shared["w_out"] = f32(inputs["w_out"])
    maps = []
    for b in range(ncores):
        m = dict(shared)
        m["x"] = np.ascontiguousarray(x[b])
        m["c_fm"] = np.ascontiguousarray(c[b].reshape(8, 128).T)
        maps.append(m)
    return maps


_PROG_CACHE = {}


def kernel(**inputs):
    x = np.asarray(inputs["x"])
    B, S, _ = x.shape
    if S not in _PROG_CACHE:
        _PROG_CACHE[S] = Prog(S).build()
    nc = _PROG_CACHE[S]
    maps = make_in_maps(inputs, S, B)
    res = run_bass_kernel_spmd(nc, maps, core_ids=list(range(B)))
    out = np.stack([np.asarray(r["out"], dtype=np.float32) for r in res.results], axis=0)
    return out
```
